# Optimizing a Trainium2 kernel written in Bass

```python
import jax, jax.numpy as jnp
from jax import lax
import numpy as np

D_MODEL = 1024
BATCH = 8
SEQ = 4096
DEPTH = 4

CHUNK = 64
RET_HEADS = 8
RET_QK_DIM = 64
RET_V_DIM = 128
RET_QK_WIDTH = RET_HEADS * RET_QK_DIM
RET_V_WIDTH = RET_HEADS * RET_V_DIM
CONV_WIDTH = D_MODEL
CONV_KERNEL = 31
ROPE_BASE = 10000.0
NORM_EPS = 1e-6
IN_SIZES = (
    RET_QK_WIDTH,
    RET_QK_WIDTH,
    RET_V_WIDTH,
    RET_V_WIDTH,
    CONV_WIDTH,
    CONV_WIDTH,
    CONV_WIDTH,
    D_MODEL,
    D_MODEL,
)
IN_WIDTH = sum(IN_SIZES)

kernel_name = "hybrid_retention_conformer_conv_block"


def _split_points():
    pts, acc = [], 0
    for s in IN_SIZES[:-1]:
        acc += s
        pts.append(acc)
    return pts


def rms_norm(x, g):
    xf = x.astype(jnp.float32)
    y = xf * lax.rsqrt(jnp.mean(xf * xf, axis=-1, keepdims=True) + NORM_EPS)
    return (y * g.astype(jnp.float32)).astype(x.dtype)


def layer_norm(x, g, b):
    xf = x.astype(jnp.float32)
    mu = jnp.mean(xf, axis=-1, keepdims=True)
    var = jnp.mean(jnp.square(xf - mu), axis=-1, keepdims=True)
    y = (xf - mu) * lax.rsqrt(var + NORM_EPS)
    return (y * g.astype(jnp.float32) + b.astype(jnp.float32)).astype(x.dtype)


def rotary(t, pos):
    half = t.shape[-1] // 2
    inv = ROPE_BASE ** (-jnp.arange(half, dtype=jnp.float32) / half)
    ang = pos[:, None] * inv[None, :]
    cos = jnp.cos(ang)[None, :, None, :]
    sin = jnp.sin(ang)[None, :, None, :]
    t1, t2 = t[..., :half], t[..., half:]
    return jnp.concatenate([t1 * cos - t2 * sin, t1 * sin + t2 * cos], axis=-1)


def chunk_retention(q, k, v):
    B, S, H, dk = q.shape
    dv = v.shape[-1]
    n = S // CHUNK
    qc = q.reshape(B, n, CHUNK, H, dk)
    kc = k.reshape(B, n, CHUNK, H, dk)
    vc = v.reshape(B, n, CHUNK, H, dv)
    log_g = jnp.log1p(-jnp.exp2(-5.0 - jnp.arange(H, dtype=jnp.float32)))
    idx = jnp.arange(CHUNK, dtype=jnp.float32)
    intra_decay = jnp.exp(log_g[:, None, None] * jnp.abs(idx[:, None] - idx[None, :]))
    scores = jnp.einsum('bnahd,bnjhd->bnhaj', qc, kc) * intra_decay
    intra = jnp.einsum('bnhaj,bnjhe->bnahe', scores, vc)
    k_decay = jnp.exp(log_g[None, :] * (CHUNK - idx[:, None]))
    kv = jnp.einsum('bnjhd,jh,bnjhe->nbhde', kc, k_decay, vc)
    chunk_decay = jnp.exp(log_g * CHUNK)[None, :, None, None]

    def step(state, kv_i):
        return state * chunk_decay + kv_i, state

    _, states = lax.scan(step, jnp.zeros((B, H, dk, dv), jnp.float32), kv)
    q_decay = jnp.exp(log_g[None, :] * idx[:, None])
    cross = jnp.einsum('bnahd,ah,nbhde->bnahe', qc, q_decay, states)
    return (intra + cross).reshape(B, S, H, dv)


def causal_depthwise_conv(u, w, b):
    K = w.shape[0]
    up = jnp.pad(u, ((0, 0), (K - 1, 0), (0, 0)))
    y = lax.conv_general_dilated(up, w[:, None, :].astype(u.dtype), window_strides=(1,), padding='VALID',
                                 dimension_numbers=('NWC', 'WIO', 'NWC'),
                                 feature_group_count=u.shape[-1])
    return y + b.astype(u.dtype)


def hybrid_layer(x, pre_g, w_in, w_ret_out, conv_w, conv_b, ln_g, ln_b, w_conv_out, w_o, post_g):
    B, S, _ = x.shape
    h = rms_norm(x, pre_g)
    proj = h @ w_in
    q, k, v, g_ret, glu_a, glu_b, g_conv, m_ret, m_conv = jnp.split(proj, _split_points(), axis=-1)

    pos = jnp.arange(S, dtype=jnp.float32)
    qh = rotary(q.astype(jnp.float32).reshape(B, S, RET_HEADS, RET_QK_DIM), pos) * (RET_QK_DIM ** -0.5)
    kh = rotary(k.astype(jnp.float32).reshape(B, S, RET_HEADS, RET_QK_DIM), pos)
    vh = v.astype(jnp.float32).reshape(B, S, RET_HEADS, RET_V_DIM)
    r = chunk_retention(qh, kh, vh)
    mu = jnp.mean(r, axis=-1, keepdims=True)
    var = jnp.mean(jnp.square(r - mu), axis=-1, keepdims=True)
    r = ((r - mu) * lax.rsqrt(var + NORM_EPS)).reshape(B, S, RET_V_WIDTH).astype(x.dtype)
    ret_out = (r * jax.nn.silu(g_ret)) @ w_ret_out

    u = glu_a * jax.nn.sigmoid(glu_b)
    c = causal_depthwise_conv(u, conv_w, conv_b)
    c = jax.nn.silu(layer_norm(c, ln_g, ln_b))
    conv_out = (c * jax.nn.silu(g_conv)) @ w_conv_out

    y = jax.nn.sigmoid(m_ret) * ret_out + jax.nn.sigmoid(m_conv) * conv_out
    y = y @ w_o
    return x + rms_norm(y, post_g)


def setup_inputs(seed: int = 0) -> dict:
    key = jax.random.key(seed)
    ks = jax.random.split(key, 12)
    f = jnp.float32
    x = jax.random.normal(ks[0], (BATCH, SEQ, D_MODEL), f)
    pre_norm_g = 1.0 + 0.05 * jax.random.normal(ks[1], (DEPTH, D_MODEL), f)
    w_in = jax.random.normal(ks[2], (DEPTH, D_MODEL, IN_WIDTH), f) * D_MODEL ** -0.5
    w_ret_out = jax.random.normal(ks[3], (DEPTH, RET_V_WIDTH, D_MODEL), f) * RET_V_WIDTH ** -0.5
    conv_w = jax.random.normal(ks[4], (DEPTH, CONV_KERNEL, CONV_WIDTH), f) * CONV_KERNEL ** -0.5
    conv_b = 0.02 * jax.random.normal(ks[5], (DEPTH, CONV_WIDTH), f)
    conv_ln_g = 1.0 + 0.05 * jax.random.normal(ks[6], (DEPTH, CONV_WIDTH), f)
    conv_ln_b = 0.02 * jax.random.normal(ks[7], (DEPTH, CONV_WIDTH), f)
    w_conv_out = jax.random.normal(ks[8], (DEPTH, CONV_WIDTH, D_MODEL), f) * CONV_WIDTH ** -0.5
    w_o = jax.random.normal(ks[9], (DEPTH, D_MODEL, D_MODEL), f) * D_MODEL ** -0.5
    post_norm_g = 1.0 + 0.05 * jax.random.normal(ks[10], (DEPTH, D_MODEL), f)
    return {"x": x, "pre_norm_g": pre_norm_g, "w_in": w_in, "w_ret_out": w_ret_out,
            "conv_w": conv_w, "conv_b": conv_b, "conv_ln_g": conv_ln_g, "conv_ln_b": conv_ln_b,
            "w_conv_out": w_conv_out, "w_o": w_o, "post_norm_g": post_norm_g}


def reference(x, pre_norm_g, w_in, w_ret_out, conv_w, conv_b, conv_ln_g, conv_ln_b, w_conv_out, w_o, post_norm_g):
    h = x
    for l in range(DEPTH):
        h = hybrid_layer(h, pre_norm_g[l], w_in[l], w_ret_out[l], conv_w[l], conv_b[l],
                         conv_ln_g[l], conv_ln_b[l], w_conv_out[l], w_o[l], post_norm_g[l])
    return h
```

```python
import contextlib
import numpy as np
import concourse.bass as bass
import concourse.mybir as mybir
from concourse.bass_utils import run_bass_kernel_spmd

F32 = mybir.dt.float32
BF16 = mybir.dt.bfloat16
AF = mybir.ActivationFunctionType
ALU = mybir.AluOpType
AX = mybir.AxisListType

D = 1024
SEQ = 4096
DEPTH = 4
NCORES = 8
TT = 512
NH = 8
KCONV = 31
HALO = KCONV - 1
EPS = 1e-6
IN_W = 8192
C_Q, C_K, C_V, C_GR, C_GA, C_GB, C_GC, C_MR, C_MC = 0, 512, 1024, 2048, 3072, 4096, 5120, 6144, 7168
NWSLOT = 3
COMPUTE = ("pe", "act", "dve", "pool")


class Sched:
    def __init__(self, nc):
        self.nc = nc
        self.ops = []
        self.last_writer = {}
        self.readers = {}

    def _add(self, eng, fn, reads, writes, dma_key=None):
        idx = len(self.ops)
        deps = {}
        for k in reads:
            w = self.last_writer.get(k)
            if w is not None:
                deps[w] = True
        for k in writes:
            w = self.last_writer.get(k)
            if w is not None:
                deps.setdefault(w, False)
            for r in self.readers.get(k, ()):
                deps.setdefault(r, False)
        deps.pop(idx, None)
        self.ops.append(dict(idx=idx, eng=eng, fn=fn, deps=deps, dma_key=dma_key, signal=False, count=None))
        for k in reads:
            self.readers.setdefault(k, []).append(idx)
        for k in writes:
            self.last_writer[k] = idx
            self.readers[k] = []
        return idx

    def op(self, eng, fn, reads=(), writes=()):
        return self._add(eng, fn, tuple(reads), tuple(writes))

    def dma(self, queue, fn, reads=(), writes=(), key=None):
        return self._add(queue, fn, tuple(reads), tuple(writes), dma_key=key)

    def emit(self, final_wait_queue="sp"):
        nc = self.nc
        ops = self.ops
        for o in ops:
            for d in o["deps"]:
                ops[d]["signal"] = True
        eng_count = {e: 0 for e in COMPUTE}
        dma_count = {}
        for o in ops:
            if o["dma_key"] is not None:
                dma_count[o["dma_key"]] = dma_count.get(o["dma_key"], 0) + 16
                o["count"] = dma_count[o["dma_key"]]
            elif o["signal"]:
                eng_count[o["eng"]] += 1
                o["count"] = eng_count[o["eng"]]
        dma_keys = sorted(dma_count.keys(), key=str)
        streams = {}
        for o in ops:
            streams.setdefault(o["eng"], []).append(o)
        with contextlib.ExitStack() as st:
            esem = {e: st.enter_context(nc.semaphore("s_" + e)) for e in COMPUTE}
            dsem = {k: st.enter_context(nc.semaphore("d_%d" % i)) for i, k in enumerate(dma_keys)}
            block = st.enter_context(nc.Block())

            def run_stream(ename, eng):
                waited = {}
                for o in streams.get(ename, []):
                    need = {}
                    for d in o["deps"]:
                        a = ops[d]
                        if a["dma_key"] is not None:
                            sk = ("d", a["dma_key"])
                        else:
                            if a["eng"] == ename and ename == "pe":
                                continue
                            sk = ("e", a["eng"])
                        need[sk] = max(need.get(sk, 0), a["count"])
                    for sk, v in need.items():
                        if waited.get(sk, 0) >= v:
                            continue
                        eng.wait_ge(dsem[sk[1]] if sk[0] == "d" else esem[sk[1]], v)
                        waited[sk] = v
                    ins = o["fn"](eng)
                    if o["dma_key"] is not None:
                        ins.then_inc(dsem[o["dma_key"]], 16)
                    elif o["signal"]:
                        ins.then_inc(esem[o["eng"]], 1)
                if ename == final_wait_queue:
                    for k in dma_keys:
                        if waited.get(("d", k), 0) < dma_count[k]:
                            eng.wait_ge(dsem[k], dma_count[k])

            @block.sync
            def _(e):
                run_stream("sp", e)

            @block.tensor
            def _(e):
                run_stream("pe", e)

            @block.scalar
            def _(e):
                run_stream("act", e)

            @block.vector
            def _(e):
                run_stream("dve", e)

            @block.gpsimd
            def _(e):
                run_stream("pool", e)
        return dict(n_ops=len(ops), eng_count=eng_count, n_dma_sems=len(dma_keys))


def _gammas():
    h = np.arange(NH, dtype=np.float64)
    return 1.0 - np.exp2(-5.0 - h)


def make_consts(seq):
    half = 32
    inv = 10000.0 ** (-np.arange(half, dtype=np.float64) / half)
    pos = np.arange(seq, dtype=np.float64)
    ang = pos[:, None] * inv[None, :]
    cos, sin = np.cos(ang), np.sin(ang)
    cs = np.concatenate([cos, sin, sin, cos], axis=1).astype(np.float32)
    g = _gammas()
    p = np.arange(128, dtype=np.float64)
    dq = 0.125 * g[None, :] ** p[:, None]
    dk = g[None, :] ** (128.0 - p[:, None])
    dqk = np.concatenate([dq, dk], axis=1).astype(np.float32)
    m = p[:, None]
    a = p[None, :]
    same = (np.floor(m / 64) == np.floor(a / 64))
    earlier = (np.floor(m / 64) < np.floor(a / 64))
    dm = np.zeros((128, NH, 128), dtype=np.float64)
    for h in range(NH):
        lg = np.log(g[h])
        dd = np.where(same, np.abs(a - m), np.where(earlier, a - m, 0.0))
        val = np.exp(lg * (dd - (128.0 - m + a)))
        dm[:, h, :] = np.where(same | earlier, val, 0.0)
    order = [0, 2, 4, 6, 1, 3, 5, 7]
    dm = np.ascontiguousarray(dm[:, order, :])
    return cs, dqk, dm.astype(np.float32), [float(x) for x in g ** 128.0]


def build_nc(L, NT, stop=None):
    S_ = NT * TT
    nc = bass.Bass("TRN2", target_bir_lowering=False)
    dt_in = lambda name, shape: nc.dram_tensor(name, shape, F32, kind="ExternalInput").ap()
    x_d = dt_in("x", [S_, D])
    w_in_d = dt_in("w_in", [L, D, IN_W])
    w_ro_d = dt_in("w_ro", [L, D, D])
    w_co_d = dt_in("w_co", [L, D, D])
    w_o_d = dt_in("w_o", [L, D, D])
    vec_d = dt_in("vecs", [128, 4, L, 8])
    cw_d = dt_in("cw", [128, L, 8, KCONV])
    pg_d = dt_in("post_g", [L, D])
    cs_d = dt_in("cs", [S_, 128])
    dqk_d = dt_in("dqk", [128, 16])
    dm_d = dt_in("dmask", [128, NH, 128])
    out_d = nc.dram_tensor("out", [S_, D], F32, kind="ExternalOutput").ap()
    wb_in = nc.dram_tensor("wb_in", [L, D, IN_W], BF16).ap()
    wb_ro = nc.dram_tensor("wb_ro", [L, D, D], BF16).ap()
    wb_co = nc.dram_tensor("wb_co", [L, D, D], BF16).ap()
    wb_o = nc.dram_tensor("wb_o", [L, D, D], BF16).ap()
    _, _, _, g128 = make_consts(128)

    with contextlib.ExitStack() as st:
        def sb(name, shape, dt):
            return st.enter_context(nc.sbuf_tensor(name, shape, dt))

        def ps(name, shape, dt):
            return st.enter_context(nc.psum_tensor(name, shape, dt))

        x_sb = sb("x_sb", [128, 4, D], F32)
        hT = sb("hT", [128, 8, TT], BF16)
        hs = [sb("hs%d" % i, [128, D], BF16) for i in range(2)]
        bufA = sb("bufA", [128, 8, 512], BF16)
        bufB = sb("bufB", [128, 8, 512], BF16)
        bufC = sb("bufC", [128, 8, 512], BF16)
        bufD = sb("bufD", [128, 8, 512 + 64], BF16)
        bufE = sb("bufE", [128, 8, 512], BF16)
        qTe = sb("qTe", [128, 4, 512], BF16)
        qTo = sb("qTo", [128, 4, 512], BF16)
        cf = sb("cf", [128, 8, 512], F32)
        t1 = sb("t1", [128, 8, 512], BF16)
        smg = sb("smg", [128, 8, 512], BF16)
        ST = [sb("ST%d" % i, [128, NH, 128], BF16) for i in range(2)]
        state = sb("state", [128, L, 4, 128], F32)
        stateb = sb("stateb", [128, L, 4, 128], BF16)
        halo = sb("halo", [128, L, 8, HALO], BF16)
        diag = [sb("diag%d" % i, [128, KCONV, 128], BF16) for i in range(2)]
        sm = [sb("sm%d" % i, [128, 512], F32) for i in range(2)]
        cb = [sb("cb%d" % i, [128, 2, 512], BF16) for i in range(2)]
        mean_t = sb("mean_t", [128, 512], F32)
        rstd_t = sb("rstd_t", [128, 512], F32)
        tmpn = [sb("tmpn%d" % i, [128, 512], F32) for i in range(2)]
        tmpo = sb("tmpo", [128, D], F32)
        identf = tmpo[:, 0:128]
        junk = sb("junk", [128, D], BF16)
        wsl = [sb("wsl%d" % i, [128, 8, 512], BF16) for i in range(NWSLOT)]
        vec = sb("vec", [128, 4, L, 8], F32)
        cw = sb("cw_sb", [128, L, 8, KCONV], F32)
        pg = sb("pg", [128, 2, D], F32)
        cs_t = sb("cs_t", [128, 4, 128], F32)
        dqk = sb("dqk_sb", [128, 16], F32)
        dmask = sb("dmask_sb", [128, NH, 128], F32)
        ident = sb("ident", [128, 128], BF16)
        ones = sb("ones", [128, 128], BF16)
        mhalf = sb("mhalf", [128, 8], F32)
        epsb = sb("epsb", [128, 1], F32)
        small = sb("small", [128, 96], F32)
        pa = [ps("pa%d" % i, [128, 512], F32) for i in range(4)]
        pb = [ps("pb%d" % i, [128, 1024], F32) for i in range(2)]

        S = Sched(nc)
        ctr = dict(pa=0, pb=0, hs=0, ST=0, diag=0, sm=0, cb=0, tmpn=0, rt=0)

        def nxt(name, n):
            i = ctr[name] % n
            ctr[name] += 1
            return i

        S.dma("sp", lambda e: e.dma_start(out=vec[:], in_=vec_d), writes=["vec"], key="c_vec")
        S.dma("sp", lambda e: e.dma_start(out=cw[:], in_=cw_d), writes=["cw"], key="c_cw")
        S.dma("sp", lambda e: e.dma_start(out=dqk[:], in_=dqk_d), writes=["dqk"], key="c_dqk")
        S.dma("sp", lambda e: e.dma_start(out=dmask[:], in_=dm_d), writes=["dmask"], key="c_dm")
        S.op("pool", lambda e: e.memset(identf, 0.0), writes=["identf"])
        S.op("pool", lambda e: e.affine_select(out=identf, in_=identf, pattern=[[-1, 128]],
                                               compare_op=ALU.not_equal, fill=1.0, base=0, channel_multiplier=1),
             reads=["identf"], writes=["identf"])
        S.op("dve", lambda e: e.tensor_copy(out=ident[:], in_=identf), reads=["identf"], writes=["ident"])
        S.op("dve", lambda e: e.memset(ones[:], 1.0), writes=["ones"])
        S.op("dve", lambda e: e.memset(mhalf[:], -0.5), writes=["mhalf"])
        S.op("dve", lambda e: e.memset(epsb[:], EPS), writes=["epsb"])
        S.op("dve", lambda e: e.memset(state[:], 0.0), writes=["state%d" % l for l in range(L)])
        S.op("dve", lambda e: e.memset(stateb[:], 0.0), writes=["stateb%d" % l for l in range(L)])
        S.op("dve", lambda e: e.memset(halo[:], 0.0), writes=["halo%d" % l for l in range(L)])
        S.op("dve", lambda e: e.memset(qTe[:], 0.0), writes=["QE"])
        S.op("dve", lambda e: e.memset(qTo[:], 0.0), writes=["QO"])

        cast_q = []

        def cast_layer(l):
            for kc in range(8):
                cast_q.append((lambda e, l=l, kc=kc: e.dma_start(out=wb_in[l, kc * 128:(kc + 1) * 128, :],
                                                                in_=w_in_d[l, kc * 128:(kc + 1) * 128, :]),
                               "wb%d_in%d" % (l, kc), "cast%d_in%d" % (l, kc)))
            for (nm, src, dst) in (("ro", w_ro_d, wb_ro), ("co", w_co_d, wb_co), ("o", w_o_d, wb_o)):
                cast_q.append((lambda e, l=l, src=src, dst=dst: e.dma_start(out=dst[l], in_=src[l]),
                               "wb%d_%s" % (l, nm), "cast%d_%s" % (l, nm)))

        def cast_tick(n=1):
            for _ in range(n):
                if cast_q:
                    fn, wk, key = cast_q.pop(0)
                    S.dma("pool", fn, writes=["castchain", wk], key=key)

        cast_layer(0)
        cast_tick(100)

        wseq = []
        for t in range(NT):
            for l in range(L):
                blocks = [("in", C_Q), ("in", C_K), ("in", C_V), ("in", C_V + 512), ("in", C_GR), ("in", C_GR + 512),
                          ("in", C_MR), ("in", C_MR + 512), ("in", C_GB), ("in", C_GB + 512), ("ro", 0), ("ro", 512),
                          ("in", C_GA), ("in", C_GA + 512),
                          ("in", C_GC), ("in", C_GC + 512), ("in", C_MC), ("co", 0), ("in", C_MC + 512),
                          ("co", 512), ("o", 0), ("o", 512)]
                wseq += [(l, k, c) for (k, c) in blocks]
        wstate = dict(issued=0, used=0)
        srcs = dict(ro=wb_ro, co=wb_co, o=wb_o)

        def issue_w(i):
            l, kind, c0 = wseq[i]
            slot = i % NWSLOT
            src = wb_in if kind == "in" else srcs[kind]
            ap = src[l].rearrange("(kc p) n -> p kc n", p=128)[:, :, c0:c0 + 512]
            rk = ["wb%d_in%d" % (l, kc) for kc in range(8)] if kind == "in" else ["wb%d_%s" % (l, kind)]
            S.dma("sp", lambda e, ap=ap, slot=slot: e.dma_start(out=wsl[slot][:], in_=ap),
                  reads=rk, writes=["w%d" % slot], key="w%d" % slot)

        def next_w(expect):
            i = wstate["used"]
            assert wseq[i][1:] == expect, (wseq[i], expect)
            cast_tick()
            while wstate["issued"] < min(len(wseq), i + NWSLOT - 1):
                issue_w(wstate["issued"])
                wstate["issued"] += 1
            wstate["used"] += 1
            return wsl[i % NWSLOT], "w%d" % (i % NWSLOT)

        def rsqrt_small(col0, n):
            S.op("pool", lambda e: e.tensor_tensor(out=small[:, col0:col0 + n], in0=small[:, col0:col0 + n],
                                                   in1=mhalf[:, 0:n], op=ALU.pow),
                 reads=["small", "mhalf"], writes=["small"])

        xv_in = x_d.rearrange("(n p) d -> p n d", p=128)
        xv_out = out_d.rearrange("(n p) d -> p n d", p=128)
        csv = cs_d.rearrange("(n p) c -> p n c", p=128)
        XK = ["x0", "x1", "x2", "x3"]
        HT = ["hT0", "hT1", "hT2", "hT3"]
        CK = ["C%d" % k for k in range(8)]
        EK = ["E%d" % c for c in range(8)]
        rtmp = [tmpo[:, 0:512], tmpo[:, 512:1024]]
        RTK = ["tmpo_0", "tmpo_1"]
        qr = bufD[:, 0:4, 0:512]
        kr = bufD[:, 4:8, 0:512]
        kT = bufE[:, 4:8, :]
        vv = bufA[:].rearrange("p (tb two) n -> p tb (two n)", two=2)
        sg = bufB[:].rearrange("p (tb two) n -> p tb (two n)", two=2)
        rgT = bufC
        sgb = bufA
        sgc = bufB
        u_ext = bufD[:, :, 0:HALO + TT]
        zT = bufC
        yT = bufE

        def proj_tok(wt, wk, tb):
            pi = nxt("pa", 4)

            def f(e, tb=tb, pi=pi, wt=wt):
                for kc in range(8):
                    ins = e.matmul(pa[pi][:], lhsT=hT[:, kc, tb * 128:(tb + 1) * 128], rhs=wt[:, kc, :],
                                   start=(kc == 0), stop=(kc == 7))
                return ins
            S.op("pe", f, reads=[HT[tb], wk], writes=["pa%d" % pi])
            return pi

        def proj_feat(wt, wk, j, rhs, rkeys):
            pi = nxt("pa", 4)

            def f(e, pi=pi, wt=wt, j=j, rhs=rhs):
                for kc in range(8):
                    ins = e.matmul(pa[pi][:], lhsT=wt[:, kc, j * 128:(j + 1) * 128], rhs=rhs[:, kc, 0:TT],
                                   start=(kc == 0), stop=(kc == 7))
                return ins
            S.op("pe", f, reads=list(rkeys) + [wk], writes=["pa%d" % pi])
            return pi

        carry = {}
        for t in range(NT):
            S.dma("sp", lambda e, t=t: e.dma_start(out=x_sb[:], in_=xv_in[:, t * 4:(t + 1) * 4, :]), writes=XK, key="xl")
            S.dma("sp", lambda e, t=t: e.dma_start(out=cs_t[:], in_=csv[:, t * 4:(t + 1) * 4, :]), writes=["cs"], key="csl")
            for l in range(L):
                if t == 0 and l + 1 < L:
                    cast_tick(100)
                    cast_layer(l + 1)
                ps_ = l % 2
                S.dma("sp", lambda e, l=l, ps_=ps_: e.dma_start(out=pg[:, ps_, :], in_=pg_d[l:l + 1, :].partition_broadcast(128)),
                      writes=["pg%d" % ps_], key="c_pg%d" % ps_)
                SK = "state%d" % l
                SBK = "stateb%d" % l

                def a_stats(tb):
                    S.op("act", lambda e, tb=tb: e.activation(out=junk[:], in_=x_sb[:, tb, :], func=AF.Square,
                                                              accum_out=small[:, tb:tb + 1]),
                         reads=[XK[tb], "junk"], writes=["junk", "junkB", "sA%d" % tb])
                    S.op("dve", lambda e, tb=tb: e.tensor_scalar(out=small[:, 8 + tb:9 + tb], in0=small[:, tb:tb + 1], scalar1=1.0 / D, scalar2=EPS,
                                                             op0=ALU.mult, op1=ALU.add), reads=["sA%d" % tb], writes=["sB%d" % tb])
                    S.op("pool", lambda e, tb=tb: e.tensor_tensor(out=small[:, 8 + tb:9 + tb], in0=small[:, 8 + tb:9 + tb],
                                                              in1=mhalf[:, 0:1], op=ALU.pow),
                         reads=["sB%d" % tb, "mhalf"], writes=["sB%d" % tb])
                if l == 0:
                    for tb in range(4):
                        a_stats(tb)
                a_st = dict(carry)
                carry.clear()

                def a_hs(tb):
                    if tb in a_st:
                        return
                    hi = nxt("hs", 2)
                    S.op("dve", lambda e, tb=tb, hi=hi: e.tensor_scalar(out=hs[hi][:], in0=x_sb[:, tb, :],
                                                                        scalar1=small[:, 8 + tb:9 + tb], scalar2=None, op0=ALU.mult),
                         reads=[XK[tb], "sB%d" % tb], writes=["hs%d" % hi])
                    a_st[tb] = hi

                def a_tr(tb):
                    hi = a_st[tb]
                    pi = nxt("pa", 4)
                    ptv = pa[pi][:].bitcast(BF16).rearrange("p (k j) -> p k j", k=8)

                    def tr8(e, hi=hi, ptv=ptv):
                        for kc in range(8):
                            ins = e.transpose(out=ptv[:, kc, :], in_=hs[hi][:, kc * 128:(kc + 1) * 128], identity=ident[:])
                        return ins
                    S.op("pe", tr8, reads=["hs%d" % hi, "ident"], writes=["pa%d" % pi])
                    a_st[tb] = (pi, ptv)

                def a_ev(tb):
                    pi, ptv = a_st[tb]
                    S.op("dve", lambda e, tb=tb, ptv=ptv, l=l: e.tensor_tensor(
                        out=hT[:, :, tb * 128:(tb + 1) * 128], in0=ptv,
                        in1=vec[:, 0, l, :].unsqueeze(2).broadcast_to([128, 8, 128]), op=ALU.mult),
                        reads=["pa%d" % pi, "vec"], writes=[HT[tb]])
                a_hs(0); a_hs(1); a_tr(0); a_tr(1); a_ev(0); a_hs(2); a_tr(2); a_ev(1); a_ev(2)

                bq = {}
                for which in range(2):
                    wt, wk = next_w(("in", C_Q if which == 0 else C_K))
                    for tb in range(4):
                        if which == 0 and tb == 3:
                            a_hs(3); a_tr(3); a_ev(3)
                        pi = proj_tok(wt, wk, tb)
                        ch = which * 4 + tb
                        qf = cf[:, ch, :]
                        S.op("dve", lambda e, pi=pi, qf=qf, which=which: e.tensor_tensor(
                            out=qf.rearrange("p (h d) -> p h d", h=NH), in0=pa[pi][:].rearrange("p (h d) -> p h d", h=NH),
                            in1=dqk[:, which * 8:(which + 1) * 8].unsqueeze(2).broadcast_to([128, NH, 64]), op=ALU.mult),
                            reads=["pa%d" % pi, "dqk"], writes=["cf%d" % ch])

                def b_rot(which, tb):
                    ch = which * 4 + tb
                    qf3 = cf[:, ch, :].rearrange("p (h d) -> p h d", h=NH)
                    dst3 = (qr if which == 0 else kr)[:, tb, :].rearrange("p (h d) -> p h d", h=NH)
                    rk = "D%d" % ch
                    for part in range(2):
                        ri = nxt("rt", 2)
                        tmp3 = rtmp[ri].rearrange("p (h d) -> p h d", h=NH)
                        tab = cs_t[:, tb, part * 64:(part + 1) * 64].unsqueeze(1).broadcast_to([128, NH, 64])
                        S.op("dve", lambda e, tmp3=tmp3, qf3=qf3, tab=tab: e.tensor_tensor(out=tmp3, in0=qf3, in1=tab, op=ALU.mult),
                             reads=["cf%d" % ch, "cs"], writes=[RTK[ri]])
                        S.op("pool", lambda e, tmp3=tmp3, dst3=dst3, part=part: e.tensor_tensor(
                            out=dst3[:, :, part * 32:(part + 1) * 32], in0=tmp3[:, :, 0:32], in1=tmp3[:, :, 32:64],
                            op=(ALU.subtract if part == 0 else ALU.add)),
                            reads=[RTK[ri]], writes=[rk])

                def b_tr(which, tb):
                    ch = which * 4 + tb
                    src = qr if which == 0 else kr
                    pi2 = nxt("pa", 4)
                    ptv = pa[pi2][:].bitcast(BF16)[:, 0:512].rearrange("p (k j) -> p k j", k=4)

                    def tr4(e, ptv=ptv, src=src, tb=tb):
                        for pr in range(4):
                            ins = e.transpose(out=ptv[:, pr, :], in_=src[:, tb, pr * 128:(pr + 1) * 128], identity=ident[:])
                        return ins
                    S.op("pe", tr4, reads=["D%d" % ch, "ident"], writes=["pa%d" % pi2])
                    if which == 0:
                        S.op("act", lambda e, ptv=ptv, tb=tb: e.copy(out=qTe[0:64, :, tb * 128:(tb + 1) * 128], in_=ptv[0:64]),
                             reads=["pa%d" % pi2], writes=["QE"])
                        S.op("act", lambda e, ptv=ptv, tb=tb: e.copy(out=qTo[64:128, :, tb * 128:(tb + 1) * 128], in_=ptv[64:128]),
                             reads=["pa%d" % pi2], writes=["QO"])
                    else:
                        S.op("act", lambda e, ptv=ptv, tb=tb: e.copy(out=kT[:, :, tb * 128:(tb + 1) * 128], in_=ptv),
                             reads=["pa%d" % pi2], writes=["E4", "E5", "E6", "E7"])

                def b_vg(which, hf, wt, wk, tb):
                    pi = proj_tok(wt, wk, tb)
                    if which == 0:
                        S.op("act", lambda e, pi=pi, tb=tb, hf=hf: e.copy(out=vv[:, tb, hf * 512:(hf + 1) * 512], in_=pa[pi][:]),
                             reads=["pa%d" % pi], writes=["A%d" % (tb * 2 + hf)])
                    else:
                        S.op("act", lambda e, pi=pi, tb=tb, hf=hf: e.activation(out=sg[:, tb, hf * 512:(hf + 1) * 512], in_=pa[pi][:], func=AF.Silu),
                             reads=["pa%d" % pi], writes=["B%d" % (tb * 2 + hf)])

                for tb in range(4):
                    b_rot(0, tb)
                wv0, wv0k = next_w(("in", C_V))
                for tb in range(4):
                    b_vg(0, 0, wv0, wv0k, tb)
                for tb in range(4):
                    b_rot(1, tb)
                wv1, wv1k = next_w(("in", C_V + 512))
                for tb in range(4):
                    b_vg(0, 1, wv1, wv1k, tb)
                for tb in range(4):
                    b_tr(0, tb)
                for tb in range(4):
                    b_tr(1, tb)

                c_st = {}

                def c_scores(tb):
                    si = nxt("ST", 2)
                    tsl = slice(tb * 128, (tb + 1) * 128)
                    kvb = []
                    for hg in range(2):
                        pi = nxt("pa", 4)

                        def sc(e, pi=pi, hg=hg, tsl=tsl):
                            qm = qTe if hg == 0 else qTo
                            for pr in range(4):
                                ins = e.matmul(pa[pi][:, pr * 128:(pr + 1) * 128], lhsT=kT[:, pr, tsl],
                                               rhs=qm[:, pr, tsl], start=True, stop=True)
                            return ins
                        S.op("pe", sc, reads=["E4", "E5", "E6", "E7", "QE", "QO"], writes=["pa%d" % pi])
                        S.op("dve", lambda e, pi=pi, hg=hg, si=si: e.tensor_tensor(
                            out=ST[si][:, hg * 4:(hg + 1) * 4, :], in0=pa[pi][:].rearrange("p (h a) -> p h a", h=4),
                            in1=dmask[:, hg * 4:(hg + 1) * 4, :], op=ALU.mult),
                            reads=["pa%d" % pi, "dmask"], writes=["ST%d_%d" % (si, hg)])
                    c_st[tb] = dict(si=si, tsl=tsl)

                def c_out(tb):
                    si, tsl = c_st[tb]["si"], c_st[tb]["tsl"]
                    bi = nxt("pb", 2)
                    pbv = pb[bi][:].rearrange("p (h e) -> p h e", h=NH)

                    def outmm(e, pbv=pbv, si=si, tb=tb, tsl=tsl, l=l):
                        for h in range(NH):
                            pr = h // 2
                            e.matmul(pbv[:, h, :], lhsT=ST[si][:, (h % 2) * 4 + h // 2, :], rhs=vv[:, tb, h * 128:(h + 1) * 128],
                                     start=True, stop=False)
                            ins = e.matmul(pbv[:, h, :], lhsT=(qTe if h % 2 == 0 else qTo)[:, pr, tsl], rhs=stateb[:, l, pr, :],
                                           start=False, stop=True)
                        return ins
                    S.op("pe", outmm, reads=["ST%d_0" % si, "ST%d_1" % si, "A%d" % (tb * 2), "A%d" % (tb * 2 + 1), "QE", "QO", SBK],
                         writes=["pb%d" % bi])
                    rsb = cf[:, 2 * tb:2 * tb + 2, :].rearrange("p a n -> p (a n)")
                    rkeys = ["cf%d" % (2 * tb), "cf%d" % (2 * tb + 1)]
                    S.op("act", lambda e, bi=bi, rsb=rsb: e.copy(out=rsb, in_=pb[bi][:]), reads=["pb%d" % bi], writes=rkeys)
                    c_st[tb].update(rsb=rsb, rkeys=rkeys, rv=rsb.rearrange("p (h e) -> p h e", h=NH))

                def c_state(tb):
                    for pg2 in range(2):
                        pk = nxt("pa", 4)

                        def kvmm(e, pk=pk, pg2=pg2, tb=tb):
                            for q in range(2):
                                pr = pg2 * 2 + q
                                ins = e.matmul(pa[pk][:, q * 256:(q + 1) * 256], lhsT=kr[:, tb, pr * 128:(pr + 1) * 128],
                                               rhs=vv[:, tb, pr * 256:(pr + 1) * 256], start=True, stop=True)
                            return ins
                        S.op("pe", kvmm, reads=["D%d" % (4 + tb), "A%d" % (tb * 2), "A%d" % (tb * 2 + 1)], writes=["pa%d" % pk])
                        for q in range(2):
                            pr = pg2 * 2 + q
                            for hh in range(2):
                                h = pr * 2 + hh
                                prt = slice(hh * 64, (hh + 1) * 64)
                                S.op("dve", lambda e, pk=pk, q=q, pr=pr, hh=hh, prt=prt, h=h, l=l: e.scalar_tensor_tensor(
                                    out=state[prt, l, pr, :], in0=state[prt, l, pr, :], scalar=g128[h],
                                    in1=pa[pk][prt, q * 256 + hh * 128:q * 256 + (hh + 1) * 128], op0=ALU.mult, op1=ALU.add),
                                    reads=["pa%d" % pk, SK], writes=[SK])
                    S.op("act", lambda e, l=l: e.copy(out=stateb[:, l, :, :], in_=state[:, l, :, :]), reads=[SK], writes=[SBK])

                def c_epi_a(tb):
                    rsb, rkeys, rv = c_st[tb]["rsb"], c_st[tb]["rkeys"], c_st[tb]["rv"]
                    b0 = 16 if tb % 2 == 0 else 64
                    sk = "sC%d" % (tb % 2)
                    sq = t1[:, 0:2, :].rearrange("p a n -> p (a n)")
                    S.op("dve", lambda e, rv=rv, b0=b0: e.tensor_reduce(out=small[:, b0:b0 + 8], in_=rv, axis=AX.X, op=ALU.add),
                         reads=rkeys, writes=[sk])
                    S.op("act", lambda e, rsb=rsb, sq=sq: e.activation(out=sq, in_=rsb, func=AF.Square),
                         reads=rkeys, writes=["t1_0", "t1_1"])

                def c_epi_b(tb):
                    b0 = 16 if tb % 2 == 0 else 64
                    sk = "sC%d" % (tb % 2)
                    sq = t1[:, 0:2, :].rearrange("p a n -> p (a n)")
                    m_, v_, x_ = slice(b0, b0 + 8), slice(b0 + 8, b0 + 16), slice(b0 + 16, b0 + 24)
                    S.op("dve", lambda e, sq=sq, v_=v_: e.tensor_reduce(out=small[:, v_], in_=sq.rearrange("p (h e) -> p h e", h=NH),
                                                                     axis=AX.X, op=ALU.add),
                         reads=["t1_0", "t1_1", sk], writes=[sk])
                    S.op("dve", lambda e, m_=m_: e.tensor_scalar(out=small[:, m_], in0=small[:, m_], scalar1=1.0 / 128, scalar2=None, op0=ALU.mult),
                         reads=[sk], writes=[sk])
                    S.op("dve", lambda e, m_=m_, x_=x_: e.tensor_tensor(out=small[:, x_], in0=small[:, m_], in1=small[:, m_], op=ALU.mult),
                         reads=[sk], writes=[sk])
                    S.op("dve", lambda e, v_=v_, x_=x_: e.scalar_tensor_tensor(out=small[:, v_], in0=small[:, v_], scalar=1.0 / 128,
                                                                            in1=small[:, x_], op0=ALU.mult, op1=ALU.subtract),
                         reads=[sk], writes=[sk])
                    S.op("dve", lambda e, v_=v_: e.tensor_scalar(out=small[:, v_], in0=small[:, v_], scalar1=EPS, scalar2=None, op0=ALU.add),
                         reads=[sk], writes=[sk])
                    S.op("pool", lambda e, v_=v_: e.tensor_tensor(out=small[:, v_], in0=small[:, v_], in1=mhalf[:, 0:8], op=ALU.pow),
                         reads=[sk, "mhalf"], writes=[sk])
                    S.op("dve", lambda e, m_=m_, v_=v_, x_=x_: e.scalar_tensor_tensor(out=small[:, x_], in0=small[:, m_], scalar=-1.0,
                                                                                   in1=small[:, v_], op0=ALU.mult, op1=ALU.mult),
                         reads=[sk], writes=[sk])

                def c_epi_c(tb):
                    rsb, rkeys, rv = c_st[tb]["rsb"], c_st[tb]["rkeys"], c_st[tb]["rv"]
                    b0 = 16 if tb % 2 == 0 else 64
                    sk = "sC%d" % (tb % 2)
                    hi = nxt("hs", 2)
                    rg = hs[hi]
                    rgk = "hs%d" % hi
                    rn = junk[:].rearrange("p (h e) -> p h e", h=NH)
                    for h in range(NH):
                        if h < 4:
                            S.op("act", lambda e, h=h, rv=rv, rn=rn, b0=b0: e.activation(
                                out=rn[:, h, :], in_=rv[:, h, :], func=AF.Identity,
                                scale=small[:, b0 + 8 + h:b0 + 9 + h], bias=small[:, b0 + 16 + h:b0 + 17 + h]),
                                reads=rkeys + [sk, "junk"], writes=["junk"])
                        else:
                            S.op("dve", lambda e, h=h, rv=rv, rn=rn, b0=b0: e.tensor_scalar(
                                out=rn[:, h, :], in0=rv[:, h, :], scalar1=small[:, b0 + h:b0 + h + 1],
                                scalar2=small[:, b0 + 8 + h:b0 + 9 + h], op0=ALU.subtract, op1=ALU.mult),
                                reads=rkeys + [sk], writes=["junkB"])
                    S.op("dve", lambda e, rg=rg, tb=tb: e.tensor_tensor(out=rg[:], in0=junk[:], in1=sg[:, tb, :], op=ALU.mult),
                         reads=["junk", "junkB", "B%d" % (tb * 2), "B%d" % (tb * 2 + 1)], writes=[rgk])
                    c_st[tb].update(rg=rg, rgk=rgk)

                def c_tr(tb):
                    rg, rgk, tsl = c_st[tb]["rg"], c_st[tb]["rgk"], c_st[tb]["tsl"]
                    pi = nxt("pa", 4)
                    ptv = pa[pi][:].bitcast(BF16).rearrange("p (k j) -> p k j", k=8)

                    def tr8b(e, rg=rg, ptv=ptv):
                        for kc in range(8):
                            ins = e.transpose(out=ptv[:, kc, :], in_=rg[:, kc * 128:(kc + 1) * 128], identity=ident[:])
                        return ins
                    S.op("pe", tr8b, reads=[rgk, "ident"], writes=["pa%d" % pi])
                    S.op("act", lambda e, ptv=ptv, tsl=tsl: e.copy(out=rgT[:, :, tsl], in_=ptv),
                         reads=["pa%d" % pi], writes=CK)

                mg = {}

                def gate_fill(c, col0):
                    hf, j = c // 4, c % 4
                    if j == 0:
                        mg["w"] = next_w(("in", col0 + hf * 512))
                    wm, wmk = mg["w"]
                    pi = proj_feat(wm, wmk, j, hT, HT)
                    S.op("act", lambda e, pi=pi, c=c: e.copy(out=smg[:, c, :], in_=pa[pi][:]), reads=["pa%d" % pi], writes=["G%d" % c])
                GK = ["G%d" % c for c in range(8)]

                c_scores(0); c_out(0); c_state(0)
                wg0, wg0k = next_w(("in", C_GR))
                for tb in range(4):
                    b_vg(1, 0, wg0, wg0k, tb)
                c_scores(1); c_epi_a(0); c_out(1); c_state(1); c_epi_b(0)
                wg1, wg1k = next_w(("in", C_GR + 512))
                for tb in range(4):
                    b_vg(1, 1, wg1, wg1k, tb)
                c_scores(2); c_epi_a(1); c_out(2); c_state(2); c_epi_c(0); c_epi_b(1)
                for c in range(4):
                    gate_fill(c, C_MR)
                c_scores(3); c_epi_a(2); c_out(3); c_state(3); c_epi_c(1); c_epi_b(2)
                for c in range(4, 8):
                    gate_fill(c, C_MR)
                gbw = {}

                def gb_fill(c):
                    hf, j = c // 4, c % 4
                    if j == 0:
                        gbw["w"] = next_w(("in", C_GB + hf * 512))
                    wt, wk = gbw["w"]
                    pi = proj_feat(wt, wk, j, hT, HT)
                    S.op("act", lambda e, pi=pi, c=c: e.activation(out=sgb[:, c, :], in_=pa[pi][:], func=AF.Sigmoid),
                         reads=["pa%d" % pi], writes=["A%d" % c])
                c_tr(0)
                gb_fill(0); gb_fill(1)
                c_epi_a(3); c_epi_c(2); c_epi_b(3)
                gb_fill(2); gb_fill(3)
                c_tr(1); c_epi_c(3)
                gb_fill(4); gb_fill(5); gb_fill(6); gb_fill(7)
                c_tr(2); c_tr(3)
                S.op("act", lambda e: e.activation(out=smg[:].rearrange("p c n -> p (c n)"), in_=smg[:].rearrange("p c n -> p (c n)"), func=AF.Sigmoid),
                     reads=GK, writes=GK)

                for hf in range(2):
                    wr, wrk = next_w(("ro", hf * 512))
                    for j in range(4):
                        c = hf * 4 + j
                        pi2 = proj_feat(wr, wrk, j, rgT, CK)
                        S.op("dve", lambda e, pi2=pi2, c=c: e.tensor_tensor(out=t1[:, c, :], in0=pa[pi2][:], in1=smg[:, c, :], op=ALU.mult),
                             reads=["pa%d" % pi2, "G%d" % c], writes=["t1_%d" % c])

                for hf in range(2):
                    wt, wk = next_w(("in", C_GA + hf * 512))
                    for j in range(4):
                        c = hf * 4 + j
                        pi = proj_feat(wt, wk, j, hT, HT)
                        S.op("dve", lambda e, pi=pi, c=c: e.tensor_tensor(out=u_ext[:, c, HALO:HALO + TT], in0=pa[pi][:], in1=sgb[:, c, :], op=ALU.mult),
                             reads=["pa%d" % pi, "A%d" % c], writes=["D%d" % c])
                        S.op("pool", lambda e, c=c, l=l: e.tensor_copy(out=u_ext[:, c, 0:HALO], in_=halo[:, l, c, :]),
                             reads=["halo%d" % l], writes=["D%d" % c])
                DKS = ["D%d" % c for c in range(8)]
                S.op("pool", lambda e, l=l: e.tensor_copy(out=halo[:, l, :, :], in_=u_ext[:, :, TT:TT + HALO]),
                     reads=DKS, writes=["halo%d" % l])
                e_st = {}

                def e_diag(c):
                    di = nxt("diag", 2)
                    S.op("dve", lambda e, di=di, c=c, l=l: e.tensor_tensor(
                        out=diag[di][:], in0=ident[:].unsqueeze(1).broadcast_to([128, KCONV, 128]),
                        in1=cw[:, l, c, :].unsqueeze(2).broadcast_to([128, KCONV, 128]), op=ALU.mult),
                        reads=["ident", "cw"], writes=["diag%d" % di])
                    e_st[c] = dict(di=di)
                e_diag(0)
                e_diag(1)
                for hf in range(2):
                    wt, wk = next_w(("in", C_GC + hf * 512))
                    for j in range(4):
                        c = hf * 4 + j
                        pi = proj_feat(wt, wk, j, hT, HT)
                        S.op("act", lambda e, pi=pi, c=c: e.activation(out=sgc[:, c, :], in_=pa[pi][:], func=AF.Silu),
                             reads=["pa%d" % pi], writes=["B%d" % c])
                bi_ln = nxt("pb", 2)

                def e_conv(c):
                    di = e_st[c]["di"]
                    pi = nxt("pa", 4)

                    def convmm(e, pi=pi, di=di, c=c):
                        for j in range(KCONV):
                            ins = e.matmul(pa[pi][:], lhsT=diag[di][:, j, :], rhs=u_ext[:, c, j:j + TT], start=(j == 0), stop=(j == KCONV - 1))
                        return ins
                    S.op("pe", convmm, reads=["diag%d" % di, "D%d" % c], writes=["pa%d" % pi])
                    S.op("act", lambda e, pi=pi, c=c, l=l: e.activation(out=cf[:, c, :], in_=pa[pi][:], func=AF.Identity,
                                                                     bias=vec[:, 1, l, c:c + 1], scale=1.0),
                         reads=["pa%d" % pi, "vec"], writes=["cf%d" % c])
                    ci = nxt("cb", 2)
                    S.op("act", lambda e, ci=ci, c=c: e.copy(out=cb[ci][:, 0, :], in_=cf[:, c, :]), reads=["cf%d" % c], writes=["cb%d_0" % ci])
                    S.op("act", lambda e, ci=ci, c=c: e.activation(out=cb[ci][:, 1, :], in_=cf[:, c, :], func=AF.Square),
                         reads=["cf%d" % c], writes=["cb%d_1" % ci])
                    e_st[c]["ci"] = ci

                def e_stat(c):
                    ci = e_st[c]["ci"]

                    def statmm(e, ci=ci, c=c, bi=bi_ln):
                        e.matmul(pb[bi][:, 0:512], lhsT=ones[:], rhs=cb[ci][:, 0, :], start=(c == 0), stop=(c == 7))
                        return e.matmul(pb[bi][:, 512:1024], lhsT=ones[:], rhs=cb[ci][:, 1, :], start=(c == 0), stop=(c == 7))
                    S.op("pe", statmm, reads=["ones", "cb%d_0" % ci, "cb%d_1" % ci], writes=["pb%d" % bi_ln])
                for c in range(8):
                    e_conv(c)
                    if c + 2 < 8:
                        e_diag(c + 2)
                    if c >= 1:
                        e_stat(c - 1)
                e_stat(7)
                bi = bi_ln
                S.op("dve", lambda e, bi=bi: e.tensor_scalar(out=mean_t[:], in0=pb[bi][:, 0:512], scalar1=1.0 / D, scalar2=None, op0=ALU.mult),
                     reads=["pb%d" % bi], writes=["mean_t"])
                S.op("dve", lambda e: e.tensor_tensor(out=rstd_t[:], in0=mean_t[:], in1=mean_t[:], op=ALU.mult),
                     reads=["mean_t"], writes=["rstd_t"])
                S.op("dve", lambda e, bi=bi: e.scalar_tensor_tensor(out=rstd_t[:], in0=pb[bi][:, 512:1024], scalar=1.0 / D, in1=rstd_t[:],
                                                                    op0=ALU.mult, op1=ALU.subtract),
                     reads=["pb%d" % bi, "rstd_t"], writes=["rstd_t"])
                S.op("act", lambda e: e.activation(out=rstd_t[:], in_=rstd_t[:], func=AF.Ln, bias=epsb[:, 0:1], scale=1.0),
                     reads=["rstd_t", "epsb"], writes=["rstd_t"])
                S.op("act", lambda e: e.activation(out=rstd_t[:], in_=rstd_t[:], func=AF.Exp, scale=-0.5),
                     reads=["rstd_t"], writes=["rstd_t"])
                ln_st = {}

                lnb = [(tmpn[0], "tmpn0"), (tmpn[1], "tmpn1"), (sm[0], "sm0"), (sm[1], "sm1")]

                def ln_a(c):
                    tb_, tk_ = lnb[c % 4]
                    S.op("dve", lambda e, c=c, tb_=tb_: e.tensor_tensor(out=tb_[:], in0=cf[:, c, :], in1=mean_t[:], op=ALU.subtract),
                         reads=["cf%d" % c, "mean_t"], writes=[tk_])
                    S.op("dve", lambda e, tb_=tb_: e.tensor_tensor(out=tb_[:], in0=tb_[:], in1=rstd_t[:], op=ALU.mult),
                         reads=[tk_, "rstd_t"], writes=[tk_])
                    S.op("act", lambda e, c=c, tb_=tb_, l=l: e.activation(out=tb_[:], in_=tb_[:], func=AF.Silu,
                                                                       scale=vec[:, 2, l, c:c + 1], bias=vec[:, 3, l, c:c + 1]),
                         reads=[tk_, "vec"], writes=[tk_])
                    ln_st[c] = (tb_, tk_)

                def ln_b(c):
                    tb_, tk_ = ln_st[c]
                    S.op("dve", lambda e, c=c, tb_=tb_: e.tensor_tensor(out=zT[:, c, :], in0=tb_[:], in1=sgc[:, c, :], op=ALU.mult),
                         reads=[tk_, "B%d" % c], writes=["C%d" % c])
                co = {}
                bo_ = nxt("pb", 2)
                acc = [pb[bo_][:, 0:512], pb[bo_][:, 512:1024], pb[bi_ln][:, 0:512], pb[bi_ln][:, 512:1024]]
                acck = ["pb%d" % bo_, "pb%d" % bi_ln]

                def co_acc(k):
                    wc0, wc0k = co["w"]

                    def f(e, k=k, wc0=wc0):
                        for j in range(4):
                            ins = e.matmul(acc[j], lhsT=wc0[:, k, j * 128:(j + 1) * 128], rhs=zT[:, k, 0:TT], start=(k == 0), stop=(k == 7))
                        return ins
                    S.op("pe", f, reads=["C%d" % k, wc0k], writes=acck)
                def gate_sig(h):
                    v = smg[:, 4 * h:4 * h + 4, :].rearrange("p c n -> p (c n)")
                    S.op("act", lambda e, v=v: e.activation(out=v, in_=v, func=AF.Sigmoid), reads=GK[4 * h:4 * h + 4], writes=GK[4 * h:4 * h + 4])
                for c in range(8):
                    gate_fill(c, C_MC)
                    if c == 0:
                        co["w"] = next_w(("co", 0))
                    if c == 5:
                        gate_sig(0)
                    ln_a(c)
                    if c >= 2:
                        ln_b(c - 2)
                        co_acc(c - 2)
                ln_b(6)
                co_acc(6)
                ln_b(7)
                co_acc(7)
                gate_sig(1)
                for hf in range(2):
                    if hf == 1:
                        wc_, wck = next_w(("co", 512))
                    for j in range(4):
                        c = hf * 4 + j
                        smi = nxt("sm", 2)
                        if hf == 0:
                            S.op("dve", lambda e, j=j, smi=smi, c=c: e.tensor_tensor(out=sm[smi][:], in0=acc[j], in1=smg[:, c, :], op=ALU.mult),
                                 reads=acck + ["G%d" % c], writes=["sm%d" % smi])
                        else:
                            pi2 = proj_feat(wc_, wck, j, zT, CK)
                            S.op("dve", lambda e, pi2=pi2, smi=smi, c=c: e.tensor_tensor(out=sm[smi][:], in0=pa[pi2][:], in1=smg[:, c, :], op=ALU.mult),
                                 reads=["pa%d" % pi2, "G%d" % c], writes=["sm%d" % smi])
                        S.op("dve", lambda e, smi=smi, c=c: e.tensor_tensor(out=yT[:, c, :], in0=sm[smi][:], in1=t1[:, c, :], op=ALU.add),
                             reads=["sm%d" % smi, "t1_%d" % c], writes=["E%d" % c])

                wo0, wo0k = next_w(("o", 0))
                wo1, wo1k = next_w(("o", 512))
                f_st = {}

                def f_mm(tb):
                    bi = nxt("pb", 2)

                    def womm(e, bi=bi, tb=tb, wo0=wo0, wo1=wo1):
                        for hf, wt in ((0, wo0), (1, wo1)):
                            for kc in range(8):
                                ins = e.matmul(pb[bi][:, hf * 512:(hf + 1) * 512], lhsT=yT[:, kc, tb * 128:(tb + 1) * 128],
                                               rhs=wt[:, kc, :], start=(kc == 0), stop=(kc == 7))
                        return ins
                    S.op("pe", womm, reads=EK + [wo0k, wo1k], writes=["pb%d" % bi])
                    f_st[tb] = bi

                def f_post(tb):
                    bi = f_st[tb]
                    c0 = 40 + tb * 4
                    tq = tb % 2
                    tbuf = cf[:, 2 * tq:2 * tq + 2, :].rearrange("p a n -> p (a n)")
                    tkeys = ["cf%d" % (2 * tq), "cf%d" % (2 * tq + 1)]
                    S.op("act", lambda e, bi=bi, c0=c0: e.activation(out=junk[:], in_=pb[bi][:], func=AF.Square, accum_out=small[:, c0:c0 + 1]),
                         reads=["pb%d" % bi, "junk"], writes=["junk", "junkB", "sF%d" % tb])
                    S.op("dve", lambda e, c0=c0: e.tensor_scalar(out=small[:, c0 + 2:c0 + 3], in0=small[:, c0:c0 + 1], scalar1=1.0 / D, scalar2=EPS,
                                                                 op0=ALU.mult, op1=ALU.add), reads=["sF%d" % tb], writes=["sG%d" % tb])
                    S.op("pool", lambda e, c0=c0: e.tensor_tensor(out=small[:, c0 + 2:c0 + 3], in0=small[:, c0 + 2:c0 + 3], in1=mhalf[:, 0:1], op=ALU.pow),
                         reads=["sG%d" % tb, "mhalf"], writes=["sG%d" % tb])
                    S.op("dve", lambda e, bi=bi, c0=c0, ps_=ps_, tbuf=tbuf: e.scalar_tensor_tensor(
                        out=tbuf, in0=pb[bi][:], scalar=small[:, c0 + 2:c0 + 3], in1=pg[:, ps_, :], op0=ALU.mult, op1=ALU.mult),
                        reads=["pb%d" % bi, "sG%d" % tb, "pg%d" % ps_], writes=tkeys)
                    S.op("dve", lambda e, tb=tb, tbuf=tbuf: e.tensor_tensor(out=x_sb[:, tb, :], in0=x_sb[:, tb, :], in1=tbuf, op=ALU.add),
                         reads=tkeys + [XK[tb]], writes=[XK[tb]])
                nxt_stats = a_stats if l + 1 < L else (lambda tb: None)

                def early_hs(tb):
                    if l + 1 < L:
                        hi = nxt("hs", 2)
                        S.op("dve", lambda e, tb=tb, hi=hi: e.tensor_scalar(out=hs[hi][:], in0=x_sb[:, tb, :],
                                                                            scalar1=small[:, 8 + tb:9 + tb], scalar2=None, op0=ALU.mult),
                             reads=[XK[tb], "sB%d" % tb], writes=["hs%d" % hi])
                        carry[tb] = hi
                f_mm(0); f_mm(1); f_post(0); f_mm(2); f_post(1); nxt_stats(0); f_mm(3); f_post(2); nxt_stats(1); early_hs(0)
                f_post(3); nxt_stats(2); early_hs(1); nxt_stats(3)
            S.dma("sp", lambda e, t=t: e.dma_start(out=xv_out[:, t * 4:(t + 1) * 4, :], in_=x_sb[:]), reads=XK, key="xs")
        info = S.emit()
    return nc, info


_CACHE = {}


def _layout_inputs(L_sel, NT, x, pre_norm_g, w_in, w_ret_out, conv_w, conv_b, conv_ln_g, conv_ln_b, w_conv_out, w_o, post_norm_g):
    ls = list(L_sel)
    fm = lambda v: np.ascontiguousarray(v[ls].reshape(len(ls), 8, 128).transpose(2, 0, 1))
    vecs = np.ascontiguousarray(np.stack([fm(pre_norm_g), fm(conv_b), fm(conv_ln_g), fm(conv_ln_b)], axis=1))
    cwt = np.ascontiguousarray(conv_w[ls].transpose(2, 0, 1).reshape(8, 128, len(ls), KCONV).transpose(1, 2, 0, 3))
    cs, dqk, dm, _ = make_consts(NT * TT)
    shared = dict(w_in=np.ascontiguousarray(w_in[ls]), w_ro=np.ascontiguousarray(w_ret_out[ls]),
                  w_co=np.ascontiguousarray(w_conv_out[ls]), w_o=np.ascontiguousarray(w_o[ls]),
                  vecs=vecs, cw=cwt, post_g=np.ascontiguousarray(post_norm_g[ls]), cs=cs, dqk=dqk, dmask=dm)
    return shared


def run_layers(L_sel, xs, params, NT=SEQ // TT):
    key = (len(L_sel), NT)
    if key not in _CACHE:
        _CACHE[key] = build_nc(len(L_sel), NT)[0]
    nc = _CACHE[key]
    shared = _layout_inputs(L_sel, NT, None, **params)
    in_maps = [dict(shared, x=np.ascontiguousarray(xc)) for xc in xs]
    res = run_bass_kernel_spmd(nc, in_maps, core_ids=list(range(len(xs))))
    return [r["out"] for r in res.results]


FUSED = True


def kernel(x, pre_norm_g, w_in, w_ret_out, conv_w, conv_b, conv_ln_g, conv_ln_b, w_conv_out, w_o, post_norm_g):
    x = np.asarray(x, dtype=np.float32)
    params = dict(pre_norm_g=np.asarray(pre_norm_g, np.float32), w_in=np.asarray(w_in, np.float32),
                  w_ret_out=np.asarray(w_ret_out, np.float32), conv_w=np.asarray(conv_w, np.float32),
                  conv_b=np.asarray(conv_b, np.float32), conv_ln_g=np.asarray(conv_ln_g, np.float32),
                  conv_ln_b=np.asarray(conv_ln_b, np.float32), w_conv_out=np.asarray(w_conv_out, np.float32),
                  w_o=np.asarray(w_o, np.float32), post_norm_g=np.asarray(post_norm_g, np.float32))
    xs = [x[b] for b in range(x.shape[0])]
    if FUSED:
        outs = run_layers(list(range(DEPTH)), xs, params)
    else:
        for l in range(DEPTH):
            xs = run_layers([l], xs, params)
        outs = xs
    return np.stack(outs, axis=0).astype(np.float32)
```

```python
import contextlib
import numpy as np
import concourse.bass as bass
import concourse.mybir as mybir
from concourse.bass_utils import run_bass_kernel_spmd

F32 = mybir.dt.float32
BF16 = mybir.dt.bfloat16
AF = mybir.ActivationFunctionType
ALU = mybir.AluOpType
AX = mybir.AxisListType

D = 1024
SEQ = 4096
DEPTH = 4
NCORES = 8
TT = 512
NH = 8
KCONV = 31
HALO = KCONV - 1
EPS = 1e-6
IN_W = 8192
C_Q, C_K, C_V, C_GR, C_GA, C_GB, C_GC, C_MR, C_MC = 0, 512, 1024, 2048, 3072, 4096, 5120, 6144, 7168
NWSLOT = 3
COMPUTE = ("pe", "act", "dve", "pool")


class Sched:
    def __init__(self, nc):
        self.nc = nc
        self.ops = []
        self.last_writer = {}
        self.readers = {}

    def _add(self, eng, fn, reads, writes, dma_key=None):
        idx = len(self.ops)
        deps = {}
        for k in reads:
            w = self.last_writer.get(k)
            if w is not None:
                deps[w] = True
        for k in writes:
            w = self.last_writer.get(k)
            if w is not None:
                deps.setdefault(w, False)
            for r in self.readers.get(k, ()):
                deps.setdefault(r, False)
        deps.pop(idx, None)
        self.ops.append(dict(idx=idx, eng=eng, fn=fn, deps=deps, dma_key=dma_key, signal=False, count=None))
        for k in reads:
            self.readers.setdefault(k, []).append(idx)
        for k in writes:
            self.last_writer[k] = idx
            self.readers[k] = []
        return idx

    def op(self, eng, fn, reads=(), writes=()):
        return self._add(eng, fn, tuple(reads), tuple(writes))

    def dma(self, queue, fn, reads=(), writes=(), key=None):
        return self._add(queue, fn, tuple(reads), tuple(writes), dma_key=key)

    def emit(self, final_wait_queue="sp"):
        nc = self.nc
        ops = self.ops
        for o in ops:
            for d in o["deps"]:
                ops[d]["signal"] = True
        eng_count = {e: 0 for e in COMPUTE}
        dma_count = {}
        for o in ops:
            if o["dma_key"] is not None:
                dma_count[o["dma_key"]] = dma_count.get(o["dma_key"], 0) + 16
                o["count"] = dma_count[o["dma_key"]]
            elif o["signal"]:
                eng_count[o["eng"]] += 1
                o["count"] = eng_count[o["eng"]]
        dma_keys = sorted(dma_count.keys(), key=str)
        streams = {}
        for o in ops:
            streams.setdefault(o["eng"], []).append(o)
        with contextlib.ExitStack() as st:
            esem = {e: st.enter_context(nc.semaphore("s_" + e)) for e in COMPUTE}
            dsem = {k: st.enter_context(nc.semaphore("d_%d" % i)) for i, k in enumerate(dma_keys)}
            block = st.enter_context(nc.Block())

            def run_stream(ename, eng):
                waited = {}
                for o in streams.get(ename, []):
                    need = {}
                    for d in o["deps"]:
                        a = ops[d]
                        if a["dma_key"] is not None:
                            sk = ("d", a["dma_key"])
                        else:
                            if a["eng"] == ename and ename == "pe":
                                continue
                            sk = ("e", a["eng"])
                        need[sk] = max(need.get(sk, 0), a["count"])
                    for sk, v in need.items():
                        if waited.get(sk, 0) >= v:
                            continue
                        eng.wait_ge(dsem[sk[1]] if sk[0] == "d" else esem[sk[1]], v)
                        waited[sk] = v
                    ins = o["fn"](eng)
                    if o["dma_key"] is not None:
                        ins.then_inc(dsem[o["dma_key"]], 16)
                    elif o["signal"]:
                        ins.then_inc(esem[o["eng"]], 1)
                if ename == final_wait_queue:
                    for k in dma_keys:
                        if waited.get(("d", k), 0) < dma_count[k]:
                            eng.wait_ge(dsem[k], dma_count[k])

            @block.sync
            def _(e):
                run_stream("sp", e)

            @block.tensor
            def _(e):
                run_stream("pe", e)

            @block.scalar
            def _(e):
                run_stream("act", e)

            @block.vector
            def _(e):
                run_stream("dve", e)

            @block.gpsimd
            def _(e):
                run_stream("pool", e)
        return dict(n_ops=len(ops), eng_count=eng_count, n_dma_sems=len(dma_keys))


def _gammas():
    h = np.arange(NH, dtype=np.float64)
    return 1.0 - np.exp2(-5.0 - h)


def make_consts(seq):
    half = 32
    inv = 10000.0 ** (-np.arange(half, dtype=np.float64) / half)
    pos = np.arange(seq, dtype=np.float64)
    ang = pos[:, None] * inv[None, :]
    cos, sin = np.cos(ang), np.sin(ang)
    cs = np.concatenate([cos, sin, sin, cos], axis=1).astype(np.float32)
    g = _gammas()
    p = np.arange(128, dtype=np.float64)
    dq = 0.125 * g[None, :] ** p[:, None]
    dk = g[None, :] ** (128.0 - p[:, None])
    dqk = np.concatenate([dq, dk], axis=1).astype(np.float32)
    m = p[:, None]
    a = p[None, :]
    same = (np.floor(m / 64) == np.floor(a / 64))
    earlier = (np.floor(m / 64) < np.floor(a / 64))
    dm = np.zeros((128, NH, 128), dtype=np.float64)
    for h in range(NH):
        lg = np.log(g[h])
        dd = np.where(same, np.abs(a - m), np.where(earlier, a - m, 0.0))
        val = np.exp(lg * (dd - (128.0 - m + a)))
        dm[:, h, :] = np.where(same | earlier, val, 0.0)
    order = [0, 2, 4, 6, 1, 3, 5, 7]
    dm = np.ascontiguousarray(dm[:, order, :])
    return cs, dqk, dm.astype(np.float32), [float(x) for x in g ** 128.0]


def build_nc(L, NT, stop=None):
    S_ = NT * TT
    nc = bass.Bass("TRN2", target_bir_lowering=False)
    dt_in = lambda name, shape: nc.dram_tensor(name, shape, F32, kind="ExternalInput").ap()
    x_d = dt_in("x", [S_, D])
    w_in_d = dt_in("w_in", [L, D, IN_W])
    w_ro_d = dt_in("w_ro", [L, D, D])
    w_co_d = dt_in("w_co", [L, D, D])
    w_o_d = dt_in("w_o", [L, D, D])
    vec_d = dt_in("vecs", [128, 4, L, 8])
    cw_d = dt_in("cw", [128, L, 8, KCONV])
    pg_d = dt_in("post_g", [L, D])
    cs_d = dt_in("cs", [S_, 128])
    dqk_d = dt_in("dqk", [128, 16])
    dm_d = dt_in("dmask", [128, NH, 128])
    out_d = nc.dram_tensor("out", [S_, D], F32, kind="ExternalOutput").ap()
    wb_in = nc.dram_tensor("wb_in", [L, D, IN_W], BF16).ap()
    wb_ro = nc.dram_tensor("wb_ro", [L, D, D], BF16).ap()
    wb_co = nc.dram_tensor("wb_co", [L, D, D], BF16).ap()
    wb_o = nc.dram_tensor("wb_o", [L, D, D], BF16).ap()
    _, _, _, g128 = make_consts(128)

    with contextlib.ExitStack() as st:
        def sb(name, shape, dt):
            return st.enter_context(nc.sbuf_tensor(name, shape, dt))

        def ps(name, shape, dt):
            return st.enter_context(nc.psum_tensor(name, shape, dt))

        x_sb = sb("x_sb", [128, 4, D], F32)
        hT = sb("hT", [128, 8, TT], BF16)
        hs = [sb("hs%d" % i, [128, D], BF16) for i in range(2)]
        bufA = sb("bufA", [128, 8, 512], BF16)
        bufB = sb("bufB", [128, 8, 512], BF16)
        bufC = sb("bufC", [128, 8, 512], BF16)
        bufD = sb("bufD", [128, 8, 512 + 64], BF16)
        bufE = sb("bufE", [128, 8, 512], BF16)
        qTe = sb("qTe", [128, 4, 512], BF16)
        qTo = sb("qTo", [128, 4, 512], BF16)
        cf = sb("cf", [128, 8, 512], F32)
        t1 = sb("t1", [128, 8, 512], BF16)
        smg = sb("smg", [128, 8, 512], BF16)
        ST = [sb("ST%d" % i, [128, NH, 128], BF16) for i in range(2)]
        state = sb("state", [128, L, 4, 128], F32)
        stateb = sb("stateb", [128, L, 4, 128], BF16)
        halo = sb("halo", [128, L, 8, HALO], BF16)
        diag = [sb("diag%d" % i, [128, KCONV, 128], BF16) for i in range(2)]
        sm = [sb("sm%d" % i, [128, 512], F32) for i in range(2)]
        cb = [sb("cb%d" % i, [128, 2, 512], BF16) for i in range(2)]
        mean_t = sb("mean_t", [128, 512], F32)
        rstd_t = sb("rstd_t", [128, 512], F32)
        tmpn = [sb("tmpn%d" % i, [128, 512], F32) for i in range(2)]
        tmpo = sb("tmpo", [128, D], F32)
        identf = tmpo[:, 0:128]
        junk = sb("junk", [128, D], BF16)
        wsl = [sb("wsl%d" % i, [128, 8, 512], BF16) for i in range(NWSLOT)]
        vec = sb("vec", [128, 4, L, 8], F32)
        cw = sb("cw_sb", [128, L, 8, KCONV], F32)
        pg = sb("pg", [128, 2, D], F32)
        cs_t = sb("cs_t", [128, 4, 128], F32)
        dqk = sb("dqk_sb", [128, 16], F32)
        dmask = sb("dmask_sb", [128, NH, 128], F32)
        ident = sb("ident", [128, 128], BF16)
        ones = sb("ones", [128, 128], BF16)
        mhalf = sb("mhalf", [128, 8], F32)
        epsb = sb("epsb", [128, 1], F32)
        small = sb("small", [128, 96], F32)
        pa = [ps("pa%d" % i, [128, 512], F32) for i in range(4)]
        pb = [ps("pb%d" % i, [128, 1024], F32) for i in range(2)]

        S = Sched(nc)
        ctr = dict(pa=0, pb=0, hs=0, ST=0, diag=0, sm=0, cb=0, tmpn=0, rt=0)

        def nxt(name, n):
            i = ctr[name] % n
            ctr[name] += 1
            return i

        S.dma("sp", lambda e: e.dma_start(out=vec[:], in_=vec_d), writes=["vec"], key="c_vec")
        S.dma("sp", lambda e: e.dma_start(out=cw[:], in_=cw_d), writes=["cw"], key="c_cw")
        S.dma("sp", lambda e: e.dma_start(out=dqk[:], in_=dqk_d), writes=["dqk"], key="c_dqk")
        S.dma("sp", lambda e: e.dma_start(out=dmask[:], in_=dm_d), writes=["dmask"], key="c_dm")
        S.op("pool", lambda e: e.memset(identf, 0.0), writes=["identf"])
        S.op("pool", lambda e: e.affine_select(out=identf, in_=identf, pattern=[[-1, 128]],
                                               compare_op=ALU.not_equal, fill=1.0, base=0, channel_multiplier=1),
             reads=["identf"], writes=["identf"])
        S.op("dve", lambda e: e.tensor_copy(out=ident[:], in_=identf), reads=["identf"], writes=["ident"])
        S.op("dve", lambda e: e.memset(ones[:], 1.0), writes=["ones"])
        S.op("dve", lambda e: e.memset(mhalf[:], -0.5), writes=["mhalf"])
        S.op("dve", lambda e: e.memset(epsb[:], EPS), writes=["epsb"])
        S.op("dve", lambda e: e.memset(state[:], 0.0), writes=["state%d" % l for l in range(L)])
        S.op("dve", lambda e: e.memset(stateb[:], 0.0), writes=["stateb%d" % l for l in range(L)])
        S.op("dve", lambda e: e.memset(halo[:], 0.0), writes=["halo%d" % l for l in range(L)])
        S.op("dve", lambda e: e.memset(qTe[:], 0.0), writes=["QE"])
        S.op("dve", lambda e: e.memset(qTo[:], 0.0), writes=["QO"])

        cast_q = []

        def cast_layer(l):
            for kc in range(8):
                cast_q.append((lambda e, l=l, kc=kc: e.dma_start(out=wb_in[l, kc * 128:(kc + 1) * 128, :],
                                                                in_=w_in_d[l, kc * 128:(kc + 1) * 128, :]),
                               "wb%d_in%d" % (l, kc), "cast%d_in%d" % (l, kc)))
            for (nm, src, dst) in (("ro", w_ro_d, wb_ro), ("co", w_co_d, wb_co), ("o", w_o_d, wb_o)):
                cast_q.append((lambda e, l=l, src=src, dst=dst: e.dma_start(out=dst[l], in_=src[l]),
                               "wb%d_%s" % (l, nm), "cast%d_%s" % (l, nm)))

        def cast_tick(n=1):
            for _ in range(n):
                if cast_q:
                    fn, wk, key = cast_q.pop(0)
                    S.dma("pool", fn, writes=["castchain", wk], key=key)

        cast_layer(0)
        cast_tick(100)

        wseq = []
        for t in range(NT):
            for l in range(L):
                blocks = [("in", C_Q), ("in", C_K), ("in", C_V), ("in", C_V + 512), ("in", C_GR), ("in", C_GR + 512),
                          ("in", C_MR), ("in", C_MR + 512), ("in", C_GB), ("in", C_GB + 512), ("ro", 0), ("ro", 512),
                          ("in", C_GA), ("in", C_GA + 512),
                          ("in", C_GC), ("in", C_GC + 512), ("in", C_MC), ("co", 0), ("in", C_MC + 512),
                          ("co", 512), ("o", 0), ("o", 512)]
                wseq += [(l, k, c) for (k, c) in blocks]
        wstate = dict(issued=0, used=0)
        srcs = dict(ro=wb_ro, co=wb_co, o=wb_o)

        def issue_w(i):
            l, kind, c0 = wseq[i]
            slot = i % NWSLOT
            src = wb_in if kind == "in" else srcs[kind]
            ap = src[l].rearrange("(kc p) n -> p kc n", p=128)[:, :, c0:c0 + 512]
            rk = ["wb%d_in%d" % (l, kc) for kc in range(8)] if kind == "in" else ["wb%d_%s" % (l, kind)]
            S.dma("sp", lambda e, ap=ap, slot=slot: e.dma_start(out=wsl[slot][:], in_=ap),
                  reads=rk, writes=["w%d" % slot], key="w%d" % slot)

        def next_w(expect):
            i = wstate["used"]
            assert wseq[i][1:] == expect, (wseq[i], expect)
            cast_tick()
            while wstate["issued"] < min(len(wseq), i + NWSLOT - 1):
                issue_w(wstate["issued"])
                wstate["issued"] += 1
            wstate["used"] += 1
            return wsl[i % NWSLOT], "w%d" % (i % NWSLOT)

        def rsqrt_small(col0, n):
            S.op("pool", lambda e: e.tensor_tensor(out=small[:, col0:col0 + n], in0=small[:, col0:col0 + n],
                                                   in1=mhalf[:, 0:n], op=ALU.pow),
                 reads=["small", "mhalf"], writes=["small"])

        xv_in = x_d.rearrange("(n p) d -> p n d", p=128)
        xv_out = out_d.rearrange("(n p) d -> p n d", p=128)
        csv = cs_d.rearrange("(n p) c -> p n c", p=128)
        XK = ["x0", "x1", "x2", "x3"]
        HT = ["hT0", "hT1", "hT2", "hT3"]
        CK = ["C%d" % k for k in range(8)]
        EK = ["E%d" % c for c in range(8)]
        rtmp = [tmpo[:, 0:512], tmpo[:, 512:1024]]
        RTK = ["tmpo_0", "tmpo_1"]
        qr = bufD[:, 0:4, 0:512]
        kr = bufD[:, 4:8, 0:512]
        kT = bufE[:, 4:8, :]
        vv = bufA[:].rearrange("p (tb two) n -> p tb (two n)", two=2)
        sg = bufB[:].rearrange("p (tb two) n -> p tb (two n)", two=2)
        rgT = bufC
        sgb = bufA
        sgc = bufB
        u_ext = bufD[:, :, 0:HALO + TT]
        zT = bufC
        yT = bufE

        def proj_tok(wt, wk, tb):
            pi = nxt("pa", 4)

            def f(e, tb=tb, pi=pi, wt=wt):
                for kc in range(8):
                    ins = e.matmul(pa[pi][:], lhsT=hT[:, kc, tb * 128:(tb + 1) * 128], rhs=wt[:, kc, :],
                                   start=(kc == 0), stop=(kc == 7))
                return ins
            S.op("pe", f, reads=[HT[tb], wk], writes=["pa%d" % pi])
            return pi

        def proj_feat(wt, wk, j, rhs, rkeys):
            pi = nxt("pa", 4)

            def f(e, pi=pi, wt=wt, j=j, rhs=rhs):
                for kc in range(8):
                    ins = e.matmul(pa[pi][:], lhsT=wt[:, kc, j * 128:(j + 1) * 128], rhs=rhs[:, kc, 0:TT],
                                   start=(kc == 0), stop=(kc == 7))
                return ins
            S.op("pe", f, reads=list(rkeys) + [wk], writes=["pa%d" % pi])
            return pi

        carry = {}
        for t in range(NT):
            S.dma("sp", lambda e, t=t: e.dma_start(out=x_sb[:], in_=xv_in[:, t * 4:(t + 1) * 4, :]), writes=XK, key="xl")
            S.dma("sp", lambda e, t=t: e.dma_start(out=cs_t[:], in_=csv[:, t * 4:(t + 1) * 4, :]), writes=["cs"], key="csl")
            for l in range(L):
                if t == 0 and l + 1 < L:
                    cast_tick(100)
                    cast_layer(l + 1)
                ps_ = l % 2
                S.dma("sp", lambda e, l=l, ps_=ps_: e.dma_start(out=pg[:, ps_, :], in_=pg_d[l:l + 1, :].partition_broadcast(128)),
                      writes=["pg%d" % ps_], key="c_pg%d" % ps_)
                SK = "state%d" % l
                SBK = "stateb%d" % l

                def a_stats(tb):
                    S.op("act", lambda e, tb=tb: e.activation(out=junk[:], in_=x_sb[:, tb, :], func=AF.Square,
                                                              accum_out=small[:, tb:tb + 1]),
                         reads=[XK[tb], "junk"], writes=["junk", "junkB", "sA%d" % tb])
                    S.op("dve", lambda e, tb=tb: e.tensor_scalar(out=small[:, 8 + tb:9 + tb], in0=small[:, tb:tb + 1], scalar1=1.0 / D, scalar2=EPS,
                                                             op0=ALU.mult, op1=ALU.add), reads=["sA%d" % tb], writes=["sB%d" % tb])
                    S.op("pool", lambda e, tb=tb: e.tensor_tensor(out=small[:, 8 + tb:9 + tb], in0=small[:, 8 + tb:9 + tb],
                                                              in1=mhalf[:, 0:1], op=ALU.pow),
                         reads=["sB%d" % tb, "mhalf"], writes=["sB%d" % tb])
                if l == 0:
                    for tb in range(4):
                        a_stats(tb)
                a_st = dict(carry)
                carry.clear()

                def a_hs(tb):
                    if tb in a_st:
                        return
                    hi = nxt("hs", 2)
                    S.op("dve", lambda e, tb=tb, hi=hi: e.tensor_scalar(out=hs[hi][:], in0=x_sb[:, tb, :],
                                                                        scalar1=small[:, 8 + tb:9 + tb], scalar2=None, op0=ALU.mult),
                         reads=[XK[tb], "sB%d" % tb], writes=["hs%d" % hi])
                    a_st[tb] = hi

                def a_tr(tb):
                    hi = a_st[tb]
                    pi = nxt("pa", 4)
                    ptv = pa[pi][:].bitcast(BF16).rearrange("p (k j) -> p k j", k=8)

                    def tr8(e, hi=hi, ptv=ptv):
                        for kc in range(8):
                            ins = e.transpose(out=ptv[:, kc, :], in_=hs[hi][:, kc * 128:(kc + 1) * 128], identity=ident[:])
                        return ins
                    S.op("pe", tr8, reads=["hs%d" % hi, "ident"], writes=["pa%d" % pi])
                    a_st[tb] = (pi, ptv)

                def a_ev(tb):
                    pi, ptv = a_st[tb]
                    S.op("dve", lambda e, tb=tb, ptv=ptv, l=l: e.tensor_tensor(
                        out=hT[:, :, tb * 128:(tb + 1) * 128], in0=ptv,
                        in1=vec[:, 0, l, :].unsqueeze(2).broadcast_to([128, 8, 128]), op=ALU.mult),
                        reads=["pa%d" % pi, "vec"], writes=[HT[tb]])
                a_hs(0); a_hs(1); a_tr(0); a_tr(1); a_ev(0); a_hs(2); a_tr(2); a_ev(1); a_ev(2)

                bq = {}
                for which in range(2):
                    wt, wk = next_w(("in", C_Q if which == 0 else C_K))
                    for tb in range(4):
                        if which == 0 and tb == 3:
                            a_hs(3); a_tr(3); a_ev(3)
                        pi = proj_tok(wt, wk, tb)
                        ch = which * 4 + tb
                        qf = cf[:, ch, :]
                        S.op("dve", lambda e, pi=pi, qf=qf, which=which: e.tensor_tensor(
                            out=qf.rearrange("p (h d) -> p h d", h=NH), in0=pa[pi][:].rearrange("p (h d) -> p h d", h=NH),
                            in1=dqk[:, which * 8:(which + 1) * 8].unsqueeze(2).broadcast_to([128, NH, 64]), op=ALU.mult),
                            reads=["pa%d" % pi, "dqk"], writes=["cf%d" % ch])

                def b_rot(which, tb):
                    ch = which * 4 + tb
                    qf3 = cf[:, ch, :].rearrange("p (h d) -> p h d", h=NH)
                    dst3 = (qr if which == 0 else kr)[:, tb, :].rearrange("p (h d) -> p h d", h=NH)
                    rk = "D%d" % ch
                    for part in range(2):
                        ri = nxt("rt", 2)
                        tmp3 = rtmp[ri].rearrange("p (h d) -> p h d", h=NH)
                        tab = cs_t[:, tb, part * 64:(part + 1) * 64].unsqueeze(1).broadcast_to([128, NH, 64])
                        S.op("dve", lambda e, tmp3=tmp3, qf3=qf3, tab=tab: e.tensor_tensor(out=tmp3, in0=qf3, in1=tab, op=ALU.mult),
                             reads=["cf%d" % ch, "cs"], writes=[RTK[ri]])
                        S.op("pool", lambda e, tmp3=tmp3, dst3=dst3, part=part: e.tensor_tensor(
                            out=dst3[:, :, part * 32:(part + 1) * 32], in0=tmp3[:, :, 0:32], in1=tmp3[:, :, 32:64],
                            op=(ALU.subtract if part == 0 else ALU.add)),
                            reads=[RTK[ri]], writes=[rk])

                def b_tr(which, tb):
                    ch = which * 4 + tb
                    src = qr if which == 0 else kr
                    pi2 = nxt("pa", 4)
                    ptv = pa[pi2][:].bitcast(BF16)[:, 0:512].rearrange("p (k j) -> p k j", k=4)

                    def tr4(e, ptv=ptv, src=src, tb=tb):
                        for pr in range(4):
                            ins = e.transpose(out=ptv[:, pr, :], in_=src[:, tb, pr * 128:(pr + 1) * 128], identity=ident[:])
                        return ins
                    S.op("pe", tr4, reads=["D%d" % ch, "ident"], writes=["pa%d" % pi2])
                    if which == 0:
                        S.op("act", lambda e, ptv=ptv, tb=tb: e.copy(out=qTe[0:64, :, tb * 128:(tb + 1) * 128], in_=ptv[0:64]),
                             reads=["pa%d" % pi2], writes=["QE"])
                        S.op("act", lambda e, ptv=ptv, tb=tb: e.copy(out=qTo[64:128, :, tb * 128:(tb + 1) * 128], in_=ptv[64:128]),
                             reads=["pa%d" % pi2], writes=["QO"])
                    else:
                        S.op("act", lambda e, ptv=ptv, tb=tb: e.copy(out=kT[:, :, tb * 128:(tb + 1) * 128], in_=ptv),
                             reads=["pa%d" % pi2], writes=["E4", "E5", "E6", "E7"])

                def b_vg(which, hf, wt, wk, tb):
                    pi = proj_tok(wt, wk, tb)
                    if which == 0:
                        S.op("act", lambda e, pi=pi, tb=tb, hf=hf: e.copy(out=vv[:, tb, hf * 512:(hf + 1) * 512], in_=pa[pi][:]),
                             reads=["pa%d" % pi], writes=["A%d" % (tb * 2 + hf)])
                    else:
                        S.op("act", lambda e, pi=pi, tb=tb, hf=hf: e.activation(out=sg[:, tb, hf * 512:(hf + 1) * 512], in_=pa[pi][:], func=AF.Silu),
                             reads=["pa%d" % pi], writes=["B%d" % (tb * 2 + hf)])

                for tb in range(4):
                    b_rot(0, tb)
                wv0, wv0k = next_w(("in", C_V))
                for tb in range(4):
                    b_vg(0, 0, wv0, wv0k, tb)
                for tb in range(4):
                    b_rot(1, tb)
                wv1, wv1k = next_w(("in", C_V + 512))
                for tb in range(4):
                    b_vg(0, 1, wv1, wv1k, tb)
                for tb in range(4):
                    b_tr(0, tb)
                for tb in range(4):
                    b_tr(1, tb)

                c_st = {}

                def c_scores(tb):
                    si = nxt("ST", 2)
                    tsl = slice(tb * 128, (tb + 1) * 128)
                    kvb = []
                    for hg in range(2):
                        pi = nxt("pa", 4)

                        def sc(e, pi=pi, hg=hg, tsl=tsl):
                            qm = qTe if hg == 0 else qTo
                            for pr in range(4):
                                ins = e.matmul(pa[pi][:, pr * 128:(pr + 1) * 128], lhsT=kT[:, pr, tsl],
                                               rhs=qm[:, pr, tsl], start=True, stop=True)
                            return ins
                        S.op("pe", sc, reads=["E4", "E5", "E6", "E7", "QE", "QO"], writes=["pa%d" % pi])
                        S.op("dve", lambda e, pi=pi, hg=hg, si=si: e.tensor_tensor(
                            out=ST[si][:, hg * 4:(hg + 1) * 4, :], in0=pa[pi][:].rearrange("p (h a) -> p h a", h=4),
                            in1=dmask[:, hg * 4:(hg + 1) * 4, :], op=ALU.mult),
                            reads=["pa%d" % pi, "dmask"], writes=["ST%d_%d" % (si, hg)])
                    c_st[tb] = dict(si=si, tsl=tsl)

                def c_out(tb):
                    si, tsl = c_st[tb]["si"], c_st[tb]["tsl"]
                    bi = nxt("pb", 2)
                    pbv = pb[bi][:].rearrange("p (h e) -> p h e", h=NH)

                    def outmm(e, pbv=pbv, si=si, tb=tb, tsl=tsl, l=l):
                        for h in range(NH):
                            pr = h // 2
                            e.matmul(pbv[:, h, :], lhsT=ST[si][:, (h % 2) * 4 + h // 2, :], rhs=vv[:, tb, h * 128:(h + 1) * 128],
                                     start=True, stop=False)
                            ins = e.matmul(pbv[:, h, :], lhsT=(qTe if h % 2 == 0 else qTo)[:, pr, tsl], rhs=stateb[:, l, pr, :],
                                           start=False, stop=True)
                        return ins
                    S.op("pe", outmm, reads=["ST%d_0" % si, "ST%d_1" % si, "A%d" % (tb * 2), "A%d" % (tb * 2 + 1), "QE", "QO", SBK],
                         writes=["pb%d" % bi])
                    rsb = cf[:, 2 * tb:2 * tb + 2, :].rearrange("p a n -> p (a n)")
                    rkeys = ["cf%d" % (2 * tb), "cf%d" % (2 * tb + 1)]
                    S.op("act", lambda e, bi=bi, rsb=rsb: e.copy(out=rsb, in_=pb[bi][:]), reads=["pb%d" % bi], writes=rkeys)
                    c_st[tb].update(rsb=rsb, rkeys=rkeys, rv=rsb.rearrange("p (h e) -> p h e", h=NH))

                def c_state(tb):
                    for pg2 in range(2):
                        pk = nxt("pa", 4)

                        def kvmm(e, pk=pk, pg2=pg2, tb=tb):
                            for q in range(2):
                                pr = pg2 * 2 + q
                                ins = e.matmul(pa[pk][:, q * 256:(q + 1) * 256], lhsT=kr[:, tb, pr * 128:(pr + 1) * 128],
                                               rhs=vv[:, tb, pr * 256:(pr + 1) * 256], start=True, stop=True)
                            return ins
                        S.op("pe", kvmm, reads=["D%d" % (4 + tb), "A%d" % (tb * 2), "A%d" % (tb * 2 + 1)], writes=["pa%d" % pk])
                        for q in range(2):
                            pr = pg2 * 2 + q
                            for hh in range(2):
                                h = pr * 2 + hh
                                prt = slice(hh * 64, (hh + 1) * 64)
                                S.op("dve", lambda e, pk=pk, q=q, pr=pr, hh=hh, prt=prt, h=h, l=l: e.scalar_tensor_tensor(
                                    out=state[prt, l, pr, :], in0=state[prt, l, pr, :], scalar=g128[h],
                                    in1=pa[pk][prt, q * 256 + hh * 128:q * 256 + (hh + 1) * 128], op0=ALU.mult, op1=ALU.add),
                                    reads=["pa%d" % pk, SK], writes=[SK])
                    S.op("act", lambda e, l=l: e.copy(out=stateb[:, l, :, :], in_=state[:, l, :, :]), reads=[SK], writes=[SBK])

                def c_epi_a(tb):
                    rsb, rkeys, rv = c_st[tb]["rsb"], c_st[tb]["rkeys"], c_st[tb]["rv"]
                    b0 = 16 if tb % 2 == 0 else 64
                    sk = "sC%d" % (tb % 2)
                    sq = t1[:, 0:2, :].rearrange("p a n -> p (a n)")
                    S.op("dve", lambda e, rv=rv, b0=b0: e.tensor_reduce(out=small[:, b0:b0 + 8], in_=rv, axis=AX.X, op=ALU.add),
                         reads=rkeys, writes=[sk])
                    S.op("act", lambda e, rsb=rsb, sq=sq: e.activation(out=sq, in_=rsb, func=AF.Square),
                         reads=rkeys, writes=["t1_0", "t1_1"])

                def c_epi_b(tb):
                    b0 = 16 if tb % 2 == 0 else 64
                    sk = "sC%d" % (tb % 2)
                    sq = t1[:, 0:2, :].rearrange("p a n -> p (a n)")
                    m_, v_, x_ = slice(b0, b0 + 8), slice(b0 + 8, b0 + 16), slice(b0 + 16, b0 + 24)
                    S.op("dve", lambda e, sq=sq, v_=v_: e.tensor_reduce(out=small[:, v_], in_=sq.rearrange("p (h e) -> p h e", h=NH),
                                                                     axis=AX.X, op=ALU.add),
                         reads=["t1_0", "t1_1", sk], writes=[sk])
                    S.op("dve", lambda e, m_=m_: e.tensor_scalar(out=small[:, m_], in0=small[:, m_], scalar1=1.0 / 128, scalar2=None, op0=ALU.mult),
                         reads=[sk], writes=[sk])
                    S.op("dve", lambda e, m_=m_, x_=x_: e.tensor_tensor(out=small[:, x_], in0=small[:, m_], in1=small[:, m_], op=ALU.mult),
                         reads=[sk], writes=[sk])
                    S.op("dve", lambda e, v_=v_, x_=x_: e.scalar_tensor_tensor(out=small[:, v_], in0=small[:, v_], scalar=1.0 / 128,
                                                                            in1=small[:, x_], op0=ALU.mult, op1=ALU.subtract),
                         reads=[sk], writes=[sk])
                    S.op("dve", lambda e, v_=v_: e.tensor_scalar(out=small[:, v_], in0=small[:, v_], scalar1=EPS, scalar2=None, op0=ALU.add),
                         reads=[sk], writes=[sk])
                    S.op("pool", lambda e, v_=v_: e.tensor_tensor(out=small[:, v_], in0=small[:, v_], in1=mhalf[:, 0:8], op=ALU.pow),
                         reads=[sk, "mhalf"], writes=[sk])

                def c_epi_c(tb):
                    rsb, rkeys, rv = c_st[tb]["rsb"], c_st[tb]["rkeys"], c_st[tb]["rv"]
                    b0 = 16 if tb % 2 == 0 else 64
                    sk = "sC%d" % (tb % 2)
                    m_, v_, x_ = slice(b0, b0 + 8), slice(b0 + 8, b0 + 16), slice(b0 + 16, b0 + 24)
                    S.op("dve", lambda e, m_=m_, v_=v_, x_=x_: e.scalar_tensor_tensor(out=small[:, x_], in0=small[:, m_], scalar=-1.0,
                                                                                   in1=small[:, v_], op0=ALU.mult, op1=ALU.mult),
                         reads=[sk], writes=[sk])
                    hi = nxt("hs", 2)
                    rg = hs[hi]
                    rgk = "hs%d" % hi
                    rn = junk[:].rearrange("p (h e) -> p h e", h=NH)
                    for h in range(NH):
                        if h < 4:
                            S.op("act", lambda e, h=h, rv=rv, rn=rn, b0=b0: e.activation(
                                out=rn[:, h, :], in_=rv[:, h, :], func=AF.Identity,
                                scale=small[:, b0 + 8 + h:b0 + 9 + h], bias=small[:, b0 + 16 + h:b0 + 17 + h]),
                                reads=rkeys + [sk, "junk"], writes=["junk"])
                        else:
                            S.op("dve", lambda e, h=h, rv=rv, rn=rn, b0=b0: e.tensor_scalar(
                                out=rn[:, h, :], in0=rv[:, h, :], scalar1=small[:, b0 + h:b0 + h + 1],
                                scalar2=small[:, b0 + 8 + h:b0 + 9 + h], op0=ALU.subtract, op1=ALU.mult),
                                reads=rkeys + [sk], writes=["junkB"])
                    S.op("dve", lambda e, rg=rg, tb=tb: e.tensor_tensor(out=rg[:], in0=junk[:], in1=sg[:, tb, :], op=ALU.mult),
                         reads=["junk", "junkB", "B%d" % (tb * 2), "B%d" % (tb * 2 + 1)], writes=[rgk])
                    c_st[tb].update(rg=rg, rgk=rgk)

                def c_tr(tb):
                    rg, rgk, tsl = c_st[tb]["rg"], c_st[tb]["rgk"], c_st[tb]["tsl"]
                    pi = nxt("pa", 4)
                    ptv = pa[pi][:].bitcast(BF16).rearrange("p (k j) -> p k j", k=8)

                    def tr8b(e, rg=rg, ptv=ptv):
                        for kc in range(8):
                            ins = e.transpose(out=ptv[:, kc, :], in_=rg[:, kc * 128:(kc + 1) * 128], identity=ident[:])
                        return ins
                    S.op("pe", tr8b, reads=[rgk, "ident"], writes=["pa%d" % pi])
                    S.op("act", lambda e, ptv=ptv, tsl=tsl: e.copy(out=rgT[:, :, tsl], in_=ptv),
                         reads=["pa%d" % pi], writes=CK)

                mg = {}

                def gate_fill(c, col0):
                    hf, j = c // 4, c % 4
                    if j == 0:
                        mg["w"] = next_w(("in", col0 + hf * 512))
                    wm, wmk = mg["w"]
                    pi = proj_feat(wm, wmk, j, hT, HT)
                    S.op("act", lambda e, pi=pi, c=c: e.copy(out=smg[:, c, :], in_=pa[pi][:]), reads=["pa%d" % pi], writes=["G%d" % c])
                GK = ["G%d" % c for c in range(8)]

                c_scores(0); c_out(0); c_state(0)
                wg0, wg0k = next_w(("in", C_GR))
                for tb in range(4):
                    b_vg(1, 0, wg0, wg0k, tb)
                c_scores(1); c_epi_a(0); c_out(1); c_state(1); c_epi_b(0)
                wg1, wg1k = next_w(("in", C_GR + 512))
                for tb in range(4):
                    b_vg(1, 1, wg1, wg1k, tb)
                c_scores(2); c_epi_a(1); c_out(2); c_state(2); c_epi_c(0); c_epi_b(1)
                for c in range(4):
                    gate_fill(c, C_MR)
                c_scores(3); c_epi_a(2); c_out(3); c_state(3); c_epi_c(1); c_epi_b(2)
                for c in range(4, 8):
                    gate_fill(c, C_MR)
                gbw = {}

                def gb_fill(c):
                    hf, j = c // 4, c % 4
                    if j == 0:
                        gbw["w"] = next_w(("in", C_GB + hf * 512))
                    wt, wk = gbw["w"]
                    pi = proj_feat(wt, wk, j, hT, HT)
                    S.op("act", lambda e, pi=pi, c=c: e.activation(out=sgb[:, c, :], in_=pa[pi][:], func=AF.Sigmoid),
                         reads=["pa%d" % pi], writes=["A%d" % c])
                c_tr(0)
                gb_fill(0); gb_fill(1)
                c_epi_a(3); c_epi_c(2); c_epi_b(3)
                gb_fill(2); gb_fill(3)
                c_tr(1); c_epi_c(3)
                gb_fill(4); gb_fill(5); gb_fill(6); gb_fill(7)
                c_tr(2); c_tr(3)
                S.op("act", lambda e: e.activation(out=smg[:].rearrange("p c n -> p (c n)"), in_=smg[:].rearrange("p c n -> p (c n)"), func=AF.Sigmoid),
                     reads=GK, writes=GK)

                for hf in range(2):
                    wr, wrk = next_w(("ro", hf * 512))
                    for j in range(4):
                        c = hf * 4 + j
                        pi2 = proj_feat(wr, wrk, j, rgT, CK)
                        S.op("dve", lambda e, pi2=pi2, c=c: e.tensor_tensor(out=t1[:, c, :], in0=pa[pi2][:], in1=smg[:, c, :], op=ALU.mult),
                             reads=["pa%d" % pi2, "G%d" % c], writes=["t1_%d" % c])

                for hf in range(2):
                    wt, wk = next_w(("in", C_GA + hf * 512))
                    for j in range(4):
                        c = hf * 4 + j
                        pi = proj_feat(wt, wk, j, hT, HT)
                        S.op("dve", lambda e, pi=pi, c=c: e.tensor_tensor(out=u_ext[:, c, HALO:HALO + TT], in0=pa[pi][:], in1=sgb[:, c, :], op=ALU.mult),
                             reads=["pa%d" % pi, "A%d" % c], writes=["D%d" % c])
                        S.op("pool", lambda e, c=c, l=l: e.tensor_copy(out=u_ext[:, c, 0:HALO], in_=halo[:, l, c, :]),
                             reads=["halo%d" % l], writes=["D%d" % c])
                DKS = ["D%d" % c for c in range(8)]
                S.op("pool", lambda e, l=l: e.tensor_copy(out=halo[:, l, :, :], in_=u_ext[:, :, TT:TT + HALO]),
                     reads=DKS, writes=["halo%d" % l])
                e_st = {}

                def e_diag(c):
                    di = nxt("diag", 2)
                    S.op("dve", lambda e, di=di, c=c, l=l: e.tensor_tensor(
                        out=diag[di][:], in0=ident[:].unsqueeze(1).broadcast_to([128, KCONV, 128]),
                        in1=cw[:, l, c, :].unsqueeze(2).broadcast_to([128, KCONV, 128]), op=ALU.mult),
                        reads=["ident", "cw"], writes=["diag%d" % di])
                    e_st[c] = dict(di=di)
                e_diag(0)
                e_diag(1)
                for hf in range(2):
                    wt, wk = next_w(("in", C_GC + hf * 512))
                    for j in range(4):
                        c = hf * 4 + j
                        pi = proj_feat(wt, wk, j, hT, HT)
                        S.op("act", lambda e, pi=pi, c=c: e.activation(out=sgc[:, c, :], in_=pa[pi][:], func=AF.Silu),
                             reads=["pa%d" % pi], writes=["B%d" % c])
                bi_ln = nxt("pb", 2)

                def e_conv(c):
                    di = e_st[c]["di"]
                    pi = nxt("pa", 4)

                    def convmm(e, pi=pi, di=di, c=c):
                        for j in range(KCONV):
                            ins = e.matmul(pa[pi][:], lhsT=diag[di][:, j, :], rhs=u_ext[:, c, j:j + TT], start=(j == 0), stop=(j == KCONV - 1))
                        return ins
                    S.op("pe", convmm, reads=["diag%d" % di, "D%d" % c], writes=["pa%d" % pi])
                    S.op("act", lambda e, pi=pi, c=c, l=l: e.activation(out=cf[:, c, :], in_=pa[pi][:], func=AF.Identity,
                                                                     bias=vec[:, 1, l, c:c + 1], scale=1.0),
                         reads=["pa%d" % pi, "vec"], writes=["cf%d" % c])
                    ci = nxt("cb", 2)
                    S.op("act", lambda e, ci=ci, c=c: e.copy(out=cb[ci][:, 0, :], in_=cf[:, c, :]), reads=["cf%d" % c], writes=["cb%d_0" % ci])
                    S.op("act", lambda e, ci=ci, c=c: e.activation(out=cb[ci][:, 1, :], in_=cf[:, c, :], func=AF.Square),
                         reads=["cf%d" % c], writes=["cb%d_1" % ci])
                    e_st[c]["ci"] = ci

                def e_stat(c):
                    ci = e_st[c]["ci"]

                    def statmm(e, ci=ci, c=c, bi=bi_ln):
                        e.matmul(pb[bi][:, 0:512], lhsT=ones[:], rhs=cb[ci][:, 0, :], start=(c == 0), stop=(c == 7))
                        return e.matmul(pb[bi][:, 512:1024], lhsT=ones[:], rhs=cb[ci][:, 1, :], start=(c == 0), stop=(c == 7))
                    S.op("pe", statmm, reads=["ones", "cb%d_0" % ci, "cb%d_1" % ci], writes=["pb%d" % bi_ln])
                for c in range(8):
                    e_conv(c)
                    if c + 2 < 8:
                        e_diag(c + 2)
                    if c >= 1:
                        e_stat(c - 1)
                e_stat(7)
                bi = bi_ln
                S.op("dve", lambda e, bi=bi: e.tensor_scalar(out=mean_t[:], in0=pb[bi][:, 0:512], scalar1=1.0 / D, scalar2=None, op0=ALU.mult),
                     reads=["pb%d" % bi], writes=["mean_t"])
                S.op("dve", lambda e: e.tensor_tensor(out=rstd_t[:], in0=mean_t[:], in1=mean_t[:], op=ALU.mult),
                     reads=["mean_t"], writes=["rstd_t"])
                S.op("dve", lambda e, bi=bi: e.scalar_tensor_tensor(out=rstd_t[:], in0=pb[bi][:, 512:1024], scalar=1.0 / D, in1=rstd_t[:],
                                                                    op0=ALU.mult, op1=ALU.subtract),
                     reads=["pb%d" % bi, "rstd_t"], writes=["rstd_t"])
                S.op("act", lambda e: e.activation(out=rstd_t[:], in_=rstd_t[:], func=AF.Ln, bias=epsb[:, 0:1], scale=1.0),
                     reads=["rstd_t", "epsb"], writes=["rstd_t"])
                S.op("act", lambda e: e.activation(out=rstd_t[:], in_=rstd_t[:], func=AF.Exp, scale=-0.5),
                     reads=["rstd_t"], writes=["rstd_t"])
                ln_st = {}

                lnb = [(tmpn[0], "tmpn0"), (tmpn[1], "tmpn1"), (sm[0], "sm0"), (sm[1], "sm1")]

                def ln_a(c):
                    tb_, tk_ = lnb[c % 4]
                    S.op("dve", lambda e, c=c, tb_=tb_: e.tensor_tensor(out=tb_[:], in0=cf[:, c, :], in1=mean_t[:], op=ALU.subtract),
                         reads=["cf%d" % c, "mean_t"], writes=[tk_])
                    S.op("dve", lambda e, tb_=tb_: e.tensor_tensor(out=tb_[:], in0=tb_[:], in1=rstd_t[:], op=ALU.mult),
                         reads=[tk_, "rstd_t"], writes=[tk_])
                    S.op("act", lambda e, c=c, tb_=tb_, l=l: e.activation(out=tb_[:], in_=tb_[:], func=AF.Silu,
                                                                       scale=vec[:, 2, l, c:c + 1], bias=vec[:, 3, l, c:c + 1]),
                         reads=[tk_, "vec"], writes=[tk_])
                    ln_st[c] = (tb_, tk_)

                def ln_b(c):
                    tb_, tk_ = ln_st[c]
                    S.op("dve", lambda e, c=c, tb_=tb_: e.tensor_tensor(out=zT[:, c, :], in0=tb_[:], in1=sgc[:, c, :], op=ALU.mult),
                         reads=[tk_, "B%d" % c], writes=["C%d" % c])
                co = {}
                bo_ = nxt("pb", 2)
                acc = [pb[bo_][:, 0:512], pb[bo_][:, 512:1024], pb[bi_ln][:, 0:512], pb[bi_ln][:, 512:1024]]
                acck = ["pb%d" % bo_, "pb%d" % bi_ln]

                def co_acc(k):
                    wc0, wc0k = co["w"]

                    def f(e, k=k, wc0=wc0):
                        for j in range(4):
                            ins = e.matmul(acc[j], lhsT=wc0[:, k, j * 128:(j + 1) * 128], rhs=zT[:, k, 0:TT], start=(k == 0), stop=(k == 7))
                        return ins
                    S.op("pe", f, reads=["C%d" % k, wc0k], writes=acck)
                def gate_sig(h):
                    v = smg[:, 4 * h:4 * h + 4, :].rearrange("p c n -> p (c n)")
                    S.op("act", lambda e, v=v: e.activation(out=v, in_=v, func=AF.Sigmoid), reads=GK[4 * h:4 * h + 4], writes=GK[4 * h:4 * h + 4])
                for c in range(8):
                    gate_fill(c, C_MC)
                    if c == 0:
                        co["w"] = next_w(("co", 0))
                    if c == 5:
                        gate_sig(0)
                    ln_a(c)
                    if c >= 2:
                        ln_b(c - 2)
                        co_acc(c - 2)
                ln_b(6)
                co_acc(6)
                ln_b(7)
                co_acc(7)
                gate_sig(1)
                for hf in range(2):
                    if hf == 1:
                        wc_, wck = next_w(("co", 512))
                    for j in range(4):
                        c = hf * 4 + j
                        smi = nxt("sm", 2)
                        if hf == 0:
                            S.op("dve", lambda e, j=j, smi=smi, c=c: e.tensor_tensor(out=sm[smi][:], in0=acc[j], in1=smg[:, c, :], op=ALU.mult),
                                 reads=acck + ["G%d" % c], writes=["sm%d" % smi])
                        else:
                            pi2 = proj_feat(wc_, wck, j, zT, CK)
                            S.op("dve", lambda e, pi2=pi2, smi=smi, c=c: e.tensor_tensor(out=sm[smi][:], in0=pa[pi2][:], in1=smg[:, c, :], op=ALU.mult),
                                 reads=["pa%d" % pi2, "G%d" % c], writes=["sm%d" % smi])
                        S.op("dve", lambda e, smi=smi, c=c: e.tensor_tensor(out=yT[:, c, :], in0=sm[smi][:], in1=t1[:, c, :], op=ALU.add),
                             reads=["sm%d" % smi, "t1_%d" % c], writes=["E%d" % c])

                wo0, wo0k = next_w(("o", 0))
                wo1, wo1k = next_w(("o", 512))
                f_st = {}

                def f_mm(tb):
                    bi = nxt("pb", 2)

                    def womm(e, bi=bi, tb=tb, wo0=wo0, wo1=wo1):
                        for hf, wt in ((0, wo0), (1, wo1)):
                            for kc in range(8):
                                ins = e.matmul(pb[bi][:, hf * 512:(hf + 1) * 512], lhsT=yT[:, kc, tb * 128:(tb + 1) * 128],
                                               rhs=wt[:, kc, :], start=(kc == 0), stop=(kc == 7))
                        return ins
                    S.op("pe", womm, reads=EK + [wo0k, wo1k], writes=["pb%d" % bi])
                    f_st[tb] = bi

                def f_post(tb):
                    bi = f_st[tb]
                    c0 = 40 + tb * 4
                    tq = tb % 2
                    tbuf = cf[:, 2 * tq:2 * tq + 2, :].rearrange("p a n -> p (a n)")
                    tkeys = ["cf%d" % (2 * tq), "cf%d" % (2 * tq + 1)]
                    S.op("act", lambda e, bi=bi, c0=c0: e.activation(out=junk[:], in_=pb[bi][:], func=AF.Square, accum_out=small[:, c0:c0 + 1]),
                         reads=["pb%d" % bi, "junk"], writes=["junk", "junkB", "sF%d" % tb])
                    S.op("dve", lambda e, c0=c0: e.tensor_scalar(out=small[:, c0 + 2:c0 + 3], in0=small[:, c0:c0 + 1], scalar1=1.0 / D, scalar2=EPS,
                                                                 op0=ALU.mult, op1=ALU.add), reads=["sF%d" % tb], writes=["sG%d" % tb])
                    S.op("pool", lambda e, c0=c0: e.tensor_tensor(out=small[:, c0 + 2:c0 + 3], in0=small[:, c0 + 2:c0 + 3], in1=mhalf[:, 0:1], op=ALU.pow),
                         reads=["sG%d" % tb, "mhalf"], writes=["sG%d" % tb])
                    S.op("dve", lambda e, bi=bi, c0=c0, ps_=ps_, tbuf=tbuf: e.scalar_tensor_tensor(
                        out=tbuf, in0=pb[bi][:], scalar=small[:, c0 + 2:c0 + 3], in1=pg[:, ps_, :], op0=ALU.mult, op1=ALU.mult),
                        reads=["pb%d" % bi, "sG%d" % tb, "pg%d" % ps_], writes=tkeys)
                    S.op("dve", lambda e, tb=tb, tbuf=tbuf: e.tensor_tensor(out=x_sb[:, tb, :], in0=x_sb[:, tb, :], in1=tbuf, op=ALU.add),
                         reads=tkeys + [XK[tb]], writes=[XK[tb]])
                nxt_stats = a_stats if l + 1 < L else (lambda tb: None)

                def early_hs(tb):
                    if l + 1 < L:
                        hi = nxt("hs", 2)
                        S.op("dve", lambda e, tb=tb, hi=hi: e.tensor_scalar(out=hs[hi][:], in0=x_sb[:, tb, :],
                                                                            scalar1=small[:, 8 + tb:9 + tb], scalar2=None, op0=ALU.mult),
                             reads=[XK[tb], "sB%d" % tb], writes=["hs%d" % hi])
                        carry[tb] = hi
                f_mm(0); f_mm(1); f_post(0); f_mm(2); f_post(1); nxt_stats(0); f_mm(3); f_post(2); nxt_stats(1); early_hs(0)
                f_post(3); nxt_stats(2); early_hs(1); nxt_stats(3)
            S.dma("sp", lambda e, t=t: e.dma_start(out=xv_out[:, t * 4:(t + 1) * 4, :], in_=x_sb[:]), reads=XK, key="xs")
        info = S.emit()
    return nc, info


_CACHE = {}


def _layout_inputs(L_sel, NT, x, pre_norm_g, w_in, w_ret_out, conv_w, conv_b, conv_ln_g, conv_ln_b, w_conv_out, w_o, post_norm_g):
    ls = list(L_sel)
    fm = lambda v: np.ascontiguousarray(v[ls].reshape(len(ls), 8, 128).transpose(2, 0, 1))
    vecs = np.ascontiguousarray(np.stack([fm(pre_norm_g), fm(conv_b), fm(conv_ln_g), fm(conv_ln_b)], axis=1))
    cwt = np.ascontiguousarray(conv_w[ls].transpose(2, 0, 1).reshape(8, 128, len(ls), KCONV).transpose(1, 2, 0, 3))
    cs, dqk, dm, _ = make_consts(NT * TT)
    shared = dict(w_in=np.ascontiguousarray(w_in[ls]), w_ro=np.ascontiguousarray(w_ret_out[ls]),
                  w_co=np.ascontiguousarray(w_conv_out[ls]), w_o=np.ascontiguousarray(w_o[ls]),
                  vecs=vecs, cw=cwt, post_g=np.ascontiguousarray(post_norm_g[ls]), cs=cs, dqk=dqk, dmask=dm)
    return shared


def run_layers(L_sel, xs, params, NT=SEQ // TT):
    key = (len(L_sel), NT)
    if key not in _CACHE:
        _CACHE[key] = build_nc(len(L_sel), NT)[0]
    nc = _CACHE[key]
    shared = _layout_inputs(L_sel, NT, None, **params)
    in_maps = [dict(shared, x=np.ascontiguousarray(xc)) for xc in xs]
    res = run_bass_kernel_spmd(nc, in_maps, core_ids=list(range(len(xs))))
    return [r["out"] for r in res.results]


FUSED = True


def kernel(x, pre_norm_g, w_in, w_ret_out, conv_w, conv_b, conv_ln_g, conv_ln_b, w_conv_out, w_o, post_norm_g):
    x = np.asarray(x, dtype=np.float32)
    params = dict(pre_norm_g=np.asarray(pre_norm_g, np.float32), w_in=np.asarray(w_in, np.float32),
                  w_ret_out=np.asarray(w_ret_out, np.float32), conv_w=np.asarray(conv_w, np.float32),
                  conv_b=np.asarray(conv_b, np.float32), conv_ln_g=np.asarray(conv_ln_g, np.float32),
                  conv_ln_b=np.asarray(conv_ln_b, np.float32), w_conv_out=np.asarray(w_conv_out, np.float32),
                  w_o=np.asarray(w_o, np.float32), post_norm_g=np.asarray(post_norm_g, np.float32))
    xs = [x[b] for b in range(x.shape[0])]
    if FUSED:
        outs = run_layers(list(range(DEPTH)), xs, params)
    else:
        for l in range(DEPTH):
            xs = run_layers([l], xs, params)
        outs = xs
    return np.stack(outs, axis=0).astype(np.float32)
```

```python
import contextlib
import numpy as np
import concourse.bass as bass
import concourse.mybir as mybir
from concourse.bass_utils import run_bass_kernel_spmd

F32 = mybir.dt.float32
BF16 = mybir.dt.bfloat16
AF = mybir.ActivationFunctionType
ALU = mybir.AluOpType
AX = mybir.AxisListType

D = 1024
SEQ = 4096
DEPTH = 4
NCORES = 8
TT = 512
NH = 8
KCONV = 31
HALO = KCONV - 1
EPS = 1e-6
IN_W = 8192
C_Q, C_K, C_V, C_GR, C_GA, C_GB, C_GC, C_MR, C_MC = 0, 512, 1024, 2048, 3072, 4096, 5120, 6144, 7168
NWSLOT = 3
COMPUTE = ("pe", "act", "dve", "pool")


class Sched:
    def __init__(self, nc):
        self.nc = nc
        self.ops = []
        self.last_writer = {}
        self.readers = {}

    def _add(self, eng, fn, reads, writes, dma_key=None):
        idx = len(self.ops)
        deps = {}
        for k in reads:
            w = self.last_writer.get(k)
            if w is not None:
                deps[w] = True
        for k in writes:
            w = self.last_writer.get(k)
            if w is not None:
                deps.setdefault(w, False)
            for r in self.readers.get(k, ()):
                deps.setdefault(r, False)
        deps.pop(idx, None)
        self.ops.append(dict(idx=idx, eng=eng, fn=fn, deps=deps, dma_key=dma_key, signal=False, count=None))
        for k in reads:
            self.readers.setdefault(k, []).append(idx)
        for k in writes:
            self.last_writer[k] = idx
            self.readers[k] = []
        return idx

    def op(self, eng, fn, reads=(), writes=()):
        return self._add(eng, fn, tuple(reads), tuple(writes))

    def dma(self, queue, fn, reads=(), writes=(), key=None):
        return self._add(queue, fn, tuple(reads), tuple(writes), dma_key=key)

    def emit(self, final_wait_queue="sp"):
        nc = self.nc
        ops = self.ops
        for o in ops:
            for d in o["deps"]:
                ops[d]["signal"] = True
        eng_count = {e: 0 for e in COMPUTE}
        dma_count = {}
        for o in ops:
            if o["dma_key"] is not None:
                dma_count[o["dma_key"]] = dma_count.get(o["dma_key"], 0) + 16
                o["count"] = dma_count[o["dma_key"]]
            elif o["signal"]:
                eng_count[o["eng"]] += 1
                o["count"] = eng_count[o["eng"]]
        dma_keys = sorted(dma_count.keys(), key=str)
        streams = {}
        for o in ops:
            streams.setdefault(o["eng"], []).append(o)
        with contextlib.ExitStack() as st:
            esem = {e: st.enter_context(nc.semaphore("s_" + e)) for e in COMPUTE}
            dsem = {k: st.enter_context(nc.semaphore("d_%d" % i)) for i, k in enumerate(dma_keys)}
            block = st.enter_context(nc.Block())

            def run_stream(ename, eng):
                waited = {}
                for o in streams.get(ename, []):
                    need = {}
                    for d in o["deps"]:
                        a = ops[d]
                        if a["dma_key"] is not None:
                            sk = ("d", a["dma_key"])
                        else:
                            if a["eng"] == ename and ename == "pe":
                                continue
                            sk = ("e", a["eng"])
                        need[sk] = max(need.get(sk, 0), a["count"])
                    for sk, v in need.items():
                        if waited.get(sk, 0) >= v:
                            continue
                        eng.wait_ge(dsem[sk[1]] if sk[0] == "d" else esem[sk[1]], v)
                        waited[sk] = v
                    ins = o["fn"](eng)
                    if o["dma_key"] is not None:
                        ins.then_inc(dsem[o["dma_key"]], 16)
                    elif o["signal"]:
                        ins.then_inc(esem[o["eng"]], 1)
                if ename == final_wait_queue:
                    for k in dma_keys:
                        if waited.get(("d", k), 0) < dma_count[k]:
                            eng.wait_ge(dsem[k], dma_count[k])

            @block.sync
            def _(e):
                run_stream("sp", e)

            @block.tensor
            def _(e):
                run_stream("pe", e)

            @block.scalar
            def _(e):
                run_stream("act", e)

            @block.vector
            def _(e):
                run_stream("dve", e)

            @block.gpsimd
            def _(e):
                run_stream("pool", e)
        return dict(n_ops=len(ops), eng_count=eng_count, n_dma_sems=len(dma_keys))


def _gammas():
    h = np.arange(NH, dtype=np.float64)
    return 1.0 - np.exp2(-5.0 - h)


def make_consts(seq):
    half = 32
    inv = 10000.0 ** (-np.arange(half, dtype=np.float64) / half)
    pos = np.arange(seq, dtype=np.float64)
    ang = pos[:, None] * inv[None, :]
    cos, sin = np.cos(ang), np.sin(ang)
    cs = np.concatenate([cos, sin, sin, cos], axis=1).astype(np.float32)
    g = _gammas()
    p = np.arange(128, dtype=np.float64)
    dq = 0.125 * g[None, :] ** p[:, None]
    dk = g[None, :] ** (128.0 - p[:, None])
    dqk = np.concatenate([dq, dk], axis=1).astype(np.float32)
    m = p[:, None]
    a = p[None, :]
    same = (np.floor(m / 64) == np.floor(a / 64))
    earlier = (np.floor(m / 64) < np.floor(a / 64))
    dm = np.zeros((128, NH, 128), dtype=np.float64)
    for h in range(NH):
        lg = np.log(g[h])
        dd = np.where(same, np.abs(a - m), np.where(earlier, a - m, 0.0))
        val = np.exp(lg * (dd - (128.0 - m + a)))
        dm[:, h, :] = np.where(same | earlier, val, 0.0)
    order = [0, 2, 4, 6, 1, 3, 5, 7]
    dm = np.ascontiguousarray(dm[:, order, :])
    return cs, dqk, dm.astype(np.float32), [float(x) for x in g ** 128.0]


def build_nc(L, NT, stop=None):
    S_ = NT * TT
    nc = bass.Bass("TRN2", target_bir_lowering=False)
    dt_in = lambda name, shape: nc.dram_tensor(name, shape, F32, kind="ExternalInput").ap()
    x_d = dt_in("x", [S_, D])
    w_in_d = dt_in("w_in", [L, D, IN_W])
    w_ro_d = dt_in("w_ro", [L, D, D])
    w_co_d = dt_in("w_co", [L, D, D])
    w_o_d = dt_in("w_o", [L, D, D])
    vec_d = dt_in("vecs", [128, 4, L, 8])
    cw_d = dt_in("cw", [128, L, 8, KCONV])
    pg_d = dt_in("post_g", [L, D])
    cs_d = dt_in("cs", [S_, 128])
    dqk_d = dt_in("dqk", [128, 16])
    dm_d = dt_in("dmask", [128, NH, 128])
    out_d = nc.dram_tensor("out", [S_, D], F32, kind="ExternalOutput").ap()
    wb_in = nc.dram_tensor("wb_in", [L, D, IN_W], BF16).ap()
    wb_ro = nc.dram_tensor("wb_ro", [L, D, D], BF16).ap()
    wb_co = nc.dram_tensor("wb_co", [L, D, D], BF16).ap()
    wb_o = nc.dram_tensor("wb_o", [L, D, D], BF16).ap()
    _, _, _, g128 = make_consts(128)

    with contextlib.ExitStack() as st:
        def sb(name, shape, dt):
            return st.enter_context(nc.sbuf_tensor(name, shape, dt))

        def ps(name, shape, dt):
            return st.enter_context(nc.psum_tensor(name, shape, dt))

        x_sb = sb("x_sb", [128, 4, D], F32)
        hT = sb("hT", [128, 8, TT], BF16)
        hs = [sb("hs%d" % i, [128, D], BF16) for i in range(2)]
        bufA = sb("bufA", [128, 8, 512], BF16)
        bufB = sb("bufB", [128, 8, 512], BF16)
        bufC = sb("bufC", [128, 8, 512], BF16)
        bufD = sb("bufD", [128, 8, 512 + 64], BF16)
        bufE = sb("bufE", [128, 8, 512], BF16)
        qTe = sb("qTe", [128, 4, 512], BF16)
        qTo = sb("qTo", [128, 4, 512], BF16)
        cf = sb("cf", [128, 8, 512], F32)
        t1 = sb("t1", [128, 8, 512], BF16)
        smg = sb("smg", [128, 8, 512], BF16)
        ST = [sb("ST%d" % i, [128, NH, 128], BF16) for i in range(2)]
        state = sb("state", [128, L, 4, 128], F32)
        stateb = sb("stateb", [128, L, 4, 128], BF16)
        halo = sb("halo", [128, L, 8, HALO], BF16)
        diag = [sb("diag%d" % i, [128, KCONV, 128], BF16) for i in range(2)]
        sm = [sb("sm%d" % i, [128, 512], F32) for i in range(2)]
        cb = [sb("cb%d" % i, [128, 2, 512], BF16) for i in range(2)]
        mean_t = sb("mean_t", [128, 512], F32)
        rstd_t = sb("rstd_t", [128, 512], F32)
        tmpn = [sb("tmpn%d" % i, [128, 512], F32) for i in range(2)]
        tmpo = sb("tmpo", [128, D], F32)
        identf = tmpo[:, 0:128]
        junk = sb("junk", [128, D], BF16)
        wsl = [sb("wsl%d" % i, [128, 8, 512], BF16) for i in range(NWSLOT)]
        vec = sb("vec", [128, 4, L, 8], F32)
        cw = sb("cw_sb", [128, L, 8, KCONV], F32)
        pg = sb("pg", [128, 2, D], F32)
        cs_t = sb("cs_t", [128, 4, 128], F32)
        dqk = sb("dqk_sb", [128, 16], F32)
        dmask = sb("dmask_sb", [128, NH, 128], F32)
        ident = sb("ident", [128, 128], BF16)
        ones = sb("ones", [128, 128], BF16)
        mhalf = sb("mhalf", [128, 8], F32)
        epsb = sb("epsb", [128, 1], F32)
        small = sb("small", [128, 96], F32)
        pa = [ps("pa%d" % i, [128, 512], F32) for i in range(4)]
        pb = [ps("pb%d" % i, [128, 1024], F32) for i in range(2)]

        S = Sched(nc)
        ctr = dict(pa=0, pb=0, hs=0, ST=0, diag=0, sm=0, cb=0, tmpn=0, rt=0)

        def nxt(name, n):
            i = ctr[name] % n
            ctr[name] += 1
            return i

        S.dma("sp", lambda e: e.dma_start(out=vec[:], in_=vec_d), writes=["vec"], key="c_vec")
        S.dma("sp", lambda e: e.dma_start(out=cw[:], in_=cw_d), writes=["cw"], key="c_cw")
        S.dma("sp", lambda e: e.dma_start(out=dqk[:], in_=dqk_d), writes=["dqk"], key="c_dqk")
        S.dma("sp", lambda e: e.dma_start(out=dmask[:], in_=dm_d), writes=["dmask"], key="c_dm")
        S.op("pool", lambda e: e.memset(identf, 0.0), writes=["identf"])
        S.op("pool", lambda e: e.affine_select(out=identf, in_=identf, pattern=[[-1, 128]],
                                               compare_op=ALU.not_equal, fill=1.0, base=0, channel_multiplier=1),
             reads=["identf"], writes=["identf"])
        S.op("dve", lambda e: e.tensor_copy(out=ident[:], in_=identf), reads=["identf"], writes=["ident"])
        S.op("dve", lambda e: e.memset(ones[:], 1.0), writes=["ones"])
        S.op("dve", lambda e: e.memset(mhalf[:], -0.5), writes=["mhalf"])
        S.op("dve", lambda e: e.memset(epsb[:], EPS), writes=["epsb"])
        S.op("dve", lambda e: e.memset(state[:], 0.0), writes=["state%d" % l for l in range(L)])
        S.op("dve", lambda e: e.memset(stateb[:], 0.0), writes=["stateb%d" % l for l in range(L)])
        S.op("dve", lambda e: e.memset(halo[:], 0.0), writes=["halo%d" % l for l in range(L)])
        S.op("dve", lambda e: e.memset(qTe[:], 0.0), writes=["QE"])
        S.op("dve", lambda e: e.memset(qTo[:], 0.0), writes=["QO"])

        cast_q = []

        def cast_layer(l):
            for kc in range(8):
                cast_q.append((lambda e, l=l, kc=kc: e.dma_start(out=wb_in[l, kc * 128:(kc + 1) * 128, :],
                                                                in_=w_in_d[l, kc * 128:(kc + 1) * 128, :]),
                               "wb%d_in%d" % (l, kc), "cast%d_in%d" % (l, kc)))
            for (nm, src, dst) in (("ro", w_ro_d, wb_ro), ("co", w_co_d, wb_co), ("o", w_o_d, wb_o)):
                cast_q.append((lambda e, l=l, src=src, dst=dst: e.dma_start(out=dst[l], in_=src[l]),
                               "wb%d_%s" % (l, nm), "cast%d_%s" % (l, nm)))

        def cast_tick(n=1):
            for _ in range(n):
                if cast_q:
                    fn, wk, key = cast_q.pop(0)
                    S.dma("pool", fn, writes=["castchain", wk], key=key)

        cast_layer(0)
        cast_tick(100)

        wseq = []
        for t in range(NT):
            for l in range(L):
                blocks = [("in", C_Q), ("in", C_K), ("in", C_V), ("in", C_V + 512), ("in", C_GR), ("in", C_GR + 512),
                          ("in", C_MR), ("in", C_MR + 512), ("in", C_GB), ("in", C_GB + 512), ("ro", 0), ("ro", 512),
                          ("in", C_GA), ("in", C_GA + 512),
                          ("in", C_GC), ("in", C_GC + 512), ("in", C_MC), ("in", C_MC + 512),
                          ("co", 0), ("co", 512), ("o", 0), ("o", 512)]
                wseq += [(l, k, c) for (k, c) in blocks]
        wstate = dict(issued=0, used=0)
        srcs = dict(ro=wb_ro, co=wb_co, o=wb_o)

        def issue_w(i):
            l, kind, c0 = wseq[i]
            slot = i % NWSLOT
            src = wb_in if kind == "in" else srcs[kind]
            ap = src[l].rearrange("(kc p) n -> p kc n", p=128)[:, :, c0:c0 + 512]
            rk = ["wb%d_in%d" % (l, kc) for kc in range(8)] if kind == "in" else ["wb%d_%s" % (l, kind)]
            S.dma("sp", lambda e, ap=ap, slot=slot: e.dma_start(out=wsl[slot][:], in_=ap),
                  reads=rk, writes=["w%d" % slot], key="w%d" % slot)

        def next_w(expect):
            i = wstate["used"]
            assert wseq[i][1:] == expect, (wseq[i], expect)
            cast_tick()
            while wstate["issued"] < min(len(wseq), i + NWSLOT - 1):
                issue_w(wstate["issued"])
                wstate["issued"] += 1
            wstate["used"] += 1
            return wsl[i % NWSLOT], "w%d" % (i % NWSLOT)

        def rsqrt_small(col0, n):
            S.op("pool", lambda e: e.tensor_tensor(out=small[:, col0:col0 + n], in0=small[:, col0:col0 + n],
                                                   in1=mhalf[:, 0:n], op=ALU.pow),
                 reads=["small", "mhalf"], writes=["small"])

        xv_in = x_d.rearrange("(n p) d -> p n d", p=128)
        xv_out = out_d.rearrange("(n p) d -> p n d", p=128)
        csv = cs_d.rearrange("(n p) c -> p n c", p=128)
        XK = ["x0", "x1", "x2", "x3"]
        HT = ["hT0", "hT1", "hT2", "hT3"]
        CK = ["C%d" % k for k in range(8)]
        EK = ["E%d" % c for c in range(8)]
        rtmp = [tmpo[:, 0:512], tmpo[:, 512:1024]]
        RTK = ["tmpo_0", "tmpo_1"]
        qr = bufD[:, 0:4, 0:512]
        kr = bufD[:, 4:8, 0:512]
        kT = bufE[:, 4:8, :]
        vv = bufA[:].rearrange("p (tb two) n -> p tb (two n)", two=2)
        sg = bufB[:].rearrange("p (tb two) n -> p tb (two n)", two=2)
        rgT = bufC
        sgb = bufA
        sgc = bufB
        u_ext = bufD[:, :, 0:HALO + TT]
        zT = bufC
        yT = bufE

        def proj_tok(wt, wk, tb):
            pi = nxt("pa", 4)

            def f(e, tb=tb, pi=pi, wt=wt):
                for kc in range(8):
                    ins = e.matmul(pa[pi][:], lhsT=hT[:, kc, tb * 128:(tb + 1) * 128], rhs=wt[:, kc, :],
                                   start=(kc == 0), stop=(kc == 7))
                return ins
            S.op("pe", f, reads=[HT[tb], wk], writes=["pa%d" % pi])
            return pi

        def proj_feat(wt, wk, j, rhs, rkeys):
            pi = nxt("pa", 4)

            def f(e, pi=pi, wt=wt, j=j, rhs=rhs):
                for kc in range(8):
                    ins = e.matmul(pa[pi][:], lhsT=wt[:, kc, j * 128:(j + 1) * 128], rhs=rhs[:, kc, 0:TT],
                                   start=(kc == 0), stop=(kc == 7))
                return ins
            S.op("pe", f, reads=list(rkeys) + [wk], writes=["pa%d" % pi])
            return pi

        carry = {}
        for t in range(NT):
            S.dma("sp", lambda e, t=t: e.dma_start(out=x_sb[:], in_=xv_in[:, t * 4:(t + 1) * 4, :]), writes=XK, key="xl")
            S.dma("sp", lambda e, t=t: e.dma_start(out=cs_t[:], in_=csv[:, t * 4:(t + 1) * 4, :]), writes=["cs"], key="csl")
            for l in range(L):
                if t == 0 and l + 1 < L:
                    cast_tick(100)
                    cast_layer(l + 1)
                ps_ = l % 2
                S.dma("sp", lambda e, l=l, ps_=ps_: e.dma_start(out=pg[:, ps_, :], in_=pg_d[l:l + 1, :].partition_broadcast(128)),
                      writes=["pg%d" % ps_], key="c_pg%d" % ps_)
                SK = "state%d" % l
                SBK = "stateb%d" % l

                def a_stats(tb):
                    S.op("act", lambda e, tb=tb: e.activation(out=junk[:], in_=x_sb[:, tb, :], func=AF.Square,
                                                              accum_out=small[:, tb:tb + 1]),
                         reads=[XK[tb], "junk"], writes=["junk", "junkB", "sA%d" % tb])
                    S.op("dve", lambda e, tb=tb: e.tensor_scalar(out=small[:, 8 + tb:9 + tb], in0=small[:, tb:tb + 1], scalar1=1.0 / D, scalar2=EPS,
                                                             op0=ALU.mult, op1=ALU.add), reads=["sA%d" % tb], writes=["sB%d" % tb])
                    S.op("pool", lambda e, tb=tb: e.tensor_tensor(out=small[:, 8 + tb:9 + tb], in0=small[:, 8 + tb:9 + tb],
                                                              in1=mhalf[:, 0:1], op=ALU.pow),
                         reads=["sB%d" % tb, "mhalf"], writes=["sB%d" % tb])
                if l == 0:
                    for tb in range(4):
                        a_stats(tb)
                a_st = dict(carry)
                carry.clear()

                def a_hs(tb):
                    if tb in a_st:
                        return
                    hi = nxt("hs", 2)
                    S.op("dve", lambda e, tb=tb, hi=hi: e.tensor_scalar(out=hs[hi][:], in0=x_sb[:, tb, :],
                                                                        scalar1=small[:, 8 + tb:9 + tb], scalar2=None, op0=ALU.mult),
                         reads=[XK[tb], "sB%d" % tb], writes=["hs%d" % hi])
                    a_st[tb] = hi

                def a_tr(tb):
                    hi = a_st[tb]
                    pi = nxt("pa", 4)
                    ptv = pa[pi][:].bitcast(BF16).rearrange("p (k j) -> p k j", k=8)

                    def tr8(e, hi=hi, ptv=ptv):
                        for kc in range(8):
                            ins = e.transpose(out=ptv[:, kc, :], in_=hs[hi][:, kc * 128:(kc + 1) * 128], identity=ident[:])
                        return ins
                    S.op("pe", tr8, reads=["hs%d" % hi, "ident"], writes=["pa%d" % pi])
                    a_st[tb] = (pi, ptv)

                def a_ev(tb):
                    pi, ptv = a_st[tb]
                    S.op("dve", lambda e, tb=tb, ptv=ptv, l=l: e.tensor_tensor(
                        out=hT[:, :, tb * 128:(tb + 1) * 128], in0=ptv,
                        in1=vec[:, 0, l, :].unsqueeze(2).broadcast_to([128, 8, 128]), op=ALU.mult),
                        reads=["pa%d" % pi, "vec"], writes=[HT[tb]])
                a_hs(0); a_hs(1); a_tr(0); a_tr(1); a_ev(0); a_hs(2); a_tr(2); a_ev(1); a_ev(2)

                bq = {}
                for which in range(2):
                    wt, wk = next_w(("in", C_Q if which == 0 else C_K))
                    for tb in range(4):
                        if which == 0 and tb == 3:
                            a_hs(3); a_tr(3); a_ev(3)
                        pi = proj_tok(wt, wk, tb)
                        ch = which * 4 + tb
                        qf = cf[:, ch, :]
                        S.op("dve", lambda e, pi=pi, qf=qf, which=which: e.tensor_tensor(
                            out=qf.rearrange("p (h d) -> p h d", h=NH), in0=pa[pi][:].rearrange("p (h d) -> p h d", h=NH),
                            in1=dqk[:, which * 8:(which + 1) * 8].unsqueeze(2).broadcast_to([128, NH, 64]), op=ALU.mult),
                            reads=["pa%d" % pi, "dqk"], writes=["cf%d" % ch])

                def b_rot(which, tb):
                    ch = which * 4 + tb
                    qf3 = cf[:, ch, :].rearrange("p (h d) -> p h d", h=NH)
                    dst3 = (qr if which == 0 else kr)[:, tb, :].rearrange("p (h d) -> p h d", h=NH)
                    rk = "D%d" % ch
                    for part in range(2):
                        ri = nxt("rt", 2)
                        tmp3 = rtmp[ri].rearrange("p (h d) -> p h d", h=NH)
                        tab = cs_t[:, tb, part * 64:(part + 1) * 64].unsqueeze(1).broadcast_to([128, NH, 64])
                        S.op("dve", lambda e, tmp3=tmp3, qf3=qf3, tab=tab: e.tensor_tensor(out=tmp3, in0=qf3, in1=tab, op=ALU.mult),
                             reads=["cf%d" % ch, "cs"], writes=[RTK[ri]])
                        S.op("pool", lambda e, tmp3=tmp3, dst3=dst3, part=part: e.tensor_tensor(
                            out=dst3[:, :, part * 32:(part + 1) * 32], in0=tmp3[:, :, 0:32], in1=tmp3[:, :, 32:64],
                            op=(ALU.subtract if part == 0 else ALU.add)),
                            reads=[RTK[ri]], writes=[rk])

                def b_tr(which, tb):
                    ch = which * 4 + tb
                    src = qr if which == 0 else kr
                    pi2 = nxt("pa", 4)
                    ptv = pa[pi2][:].bitcast(BF16)[:, 0:512].rearrange("p (k j) -> p k j", k=4)

                    def tr4(e, ptv=ptv, src=src, tb=tb):
                        for pr in range(4):
                            ins = e.transpose(out=ptv[:, pr, :], in_=src[:, tb, pr * 128:(pr + 1) * 128], identity=ident[:])
                        return ins
                    S.op("pe", tr4, reads=["D%d" % ch, "ident"], writes=["pa%d" % pi2])
                    if which == 0:
                        S.op("act", lambda e, ptv=ptv, tb=tb: e.copy(out=qTe[0:64, :, tb * 128:(tb + 1) * 128], in_=ptv[0:64]),
                             reads=["pa%d" % pi2], writes=["QE"])
                        S.op("act", lambda e, ptv=ptv, tb=tb: e.copy(out=qTo[64:128, :, tb * 128:(tb + 1) * 128], in_=ptv[64:128]),
                             reads=["pa%d" % pi2], writes=["QO"])
                    else:
                        S.op("act", lambda e, ptv=ptv, tb=tb: e.copy(out=kT[:, :, tb * 128:(tb + 1) * 128], in_=ptv),
                             reads=["pa%d" % pi2], writes=["E4", "E5", "E6", "E7"])

                def b_vg(which, hf, wt, wk, tb):
                    pi = proj_tok(wt, wk, tb)
                    if which == 0:
                        S.op("act", lambda e, pi=pi, tb=tb, hf=hf: e.copy(out=vv[:, tb, hf * 512:(hf + 1) * 512], in_=pa[pi][:]),
                             reads=["pa%d" % pi], writes=["A%d" % (tb * 2 + hf)])
                    else:
                        S.op("act", lambda e, pi=pi, tb=tb, hf=hf: e.activation(out=sg[:, tb, hf * 512:(hf + 1) * 512], in_=pa[pi][:], func=AF.Silu),
                             reads=["pa%d" % pi], writes=["B%d" % (tb * 2 + hf)])

                for tb in range(4):
                    b_rot(0, tb)
                wv0, wv0k = next_w(("in", C_V))
                for tb in range(4):
                    b_vg(0, 0, wv0, wv0k, tb)
                for tb in range(4):
                    b_rot(1, tb)
                wv1, wv1k = next_w(("in", C_V + 512))
                for tb in range(4):
                    b_vg(0, 1, wv1, wv1k, tb)
                for tb in range(4):
                    b_tr(0, tb)
                for tb in range(4):
                    b_tr(1, tb)

                c_st = {}

                def c_scores(tb):
                    si = nxt("ST", 2)
                    tsl = slice(tb * 128, (tb + 1) * 128)
                    kvb = []
                    for hg in range(2):
                        pi = nxt("pa", 4)

                        def sc(e, pi=pi, hg=hg, tsl=tsl):
                            qm = qTe if hg == 0 else qTo
                            for pr in range(4):
                                ins = e.matmul(pa[pi][:, pr * 128:(pr + 1) * 128], lhsT=kT[:, pr, tsl],
                                               rhs=qm[:, pr, tsl], start=True, stop=True)
                            return ins
                        S.op("pe", sc, reads=["E4", "E5", "E6", "E7", "QE", "QO"], writes=["pa%d" % pi])
                        S.op("dve", lambda e, pi=pi, hg=hg, si=si: e.tensor_tensor(
                            out=ST[si][:, hg * 4:(hg + 1) * 4, :], in0=pa[pi][:].rearrange("p (h a) -> p h a", h=4),
                            in1=dmask[:, hg * 4:(hg + 1) * 4, :], op=ALU.mult),
                            reads=["pa%d" % pi, "dmask"], writes=["ST%d_%d" % (si, hg)])
                    c_st[tb] = dict(si=si, tsl=tsl)

                def c_out(tb):
                    si, tsl = c_st[tb]["si"], c_st[tb]["tsl"]
                    bi = nxt("pb", 2)
                    pbv = pb[bi][:].rearrange("p (h e) -> p h e", h=NH)

                    def outmm(e, pbv=pbv, si=si, tb=tb, tsl=tsl, l=l):
                        for h in range(NH):
                            pr = h // 2
                            e.matmul(pbv[:, h, :], lhsT=ST[si][:, (h % 2) * 4 + h // 2, :], rhs=vv[:, tb, h * 128:(h + 1) * 128],
                                     start=True, stop=False)
                            ins = e.matmul(pbv[:, h, :], lhsT=(qTe if h % 2 == 0 else qTo)[:, pr, tsl], rhs=stateb[:, l, pr, :],
                                           start=False, stop=True)
                        return ins
                    S.op("pe", outmm, reads=["ST%d_0" % si, "ST%d_1" % si, "A%d" % (tb * 2), "A%d" % (tb * 2 + 1), "QE", "QO", SBK],
                         writes=["pb%d" % bi])
                    rsb = cf[:, 2 * tb:2 * tb + 2, :].rearrange("p a n -> p (a n)")
                    rkeys = ["cf%d" % (2 * tb), "cf%d" % (2 * tb + 1)]
                    S.op("act", lambda e, bi=bi, rsb=rsb: e.copy(out=rsb, in_=pb[bi][:]), reads=["pb%d" % bi], writes=rkeys)
                    c_st[tb].update(rsb=rsb, rkeys=rkeys, rv=rsb.rearrange("p (h e) -> p h e", h=NH))

                def c_state(tb):
                    for pg2 in range(2):
                        pk = nxt("pa", 4)

                        def kvmm(e, pk=pk, pg2=pg2, tb=tb):
                            for q in range(2):
                                pr = pg2 * 2 + q
                                ins = e.matmul(pa[pk][:, q * 256:(q + 1) * 256], lhsT=kr[:, tb, pr * 128:(pr + 1) * 128],
                                               rhs=vv[:, tb, pr * 256:(pr + 1) * 256], start=True, stop=True)
                            return ins
                        S.op("pe", kvmm, reads=["D%d" % (4 + tb), "A%d" % (tb * 2), "A%d" % (tb * 2 + 1)], writes=["pa%d" % pk])
                        for q in range(2):
                            pr = pg2 * 2 + q
                            for hh in range(2):
                                h = pr * 2 + hh
                                prt = slice(hh * 64, (hh + 1) * 64)
                                S.op("dve", lambda e, pk=pk, q=q, pr=pr, hh=hh, prt=prt, h=h, l=l: e.scalar_tensor_tensor(
                                    out=state[prt, l, pr, :], in0=state[prt, l, pr, :], scalar=g128[h],
                                    in1=pa[pk][prt, q * 256 + hh * 128:q * 256 + (hh + 1) * 128], op0=ALU.mult, op1=ALU.add),
                                    reads=["pa%d" % pk, SK], writes=[SK])
                    S.op("act", lambda e, l=l: e.copy(out=stateb[:, l, :, :], in_=state[:, l, :, :]), reads=[SK], writes=[SBK])

                def c_epi_a(tb):
                    rsb, rkeys, rv = c_st[tb]["rsb"], c_st[tb]["rkeys"], c_st[tb]["rv"]
                    b0 = 16 if tb % 2 == 0 else 64
                    sk = "sC%d" % (tb % 2)
                    sq = t1[:, 0:2, :].rearrange("p a n -> p (a n)")
                    S.op("dve", lambda e, rv=rv, b0=b0: e.tensor_reduce(out=small[:, b0:b0 + 8], in_=rv, axis=AX.X, op=ALU.add),
                         reads=rkeys, writes=[sk])
                    S.op("act", lambda e, rsb=rsb, sq=sq: e.activation(out=sq, in_=rsb, func=AF.Square),
                         reads=rkeys, writes=["t1_0", "t1_1"])

                def c_epi_b(tb):
                    b0 = 16 if tb % 2 == 0 else 64
                    sk = "sC%d" % (tb % 2)
                    sq = t1[:, 0:2, :].rearrange("p a n -> p (a n)")
                    m_, v_, x_ = slice(b0, b0 + 8), slice(b0 + 8, b0 + 16), slice(b0 + 16, b0 + 24)
                    S.op("dve", lambda e, sq=sq, v_=v_: e.tensor_reduce(out=small[:, v_], in_=sq.rearrange("p (h e) -> p h e", h=NH),
                                                                     axis=AX.X, op=ALU.add),
                         reads=["t1_0", "t1_1", sk], writes=[sk])
                    S.op("dve", lambda e, m_=m_: e.tensor_scalar(out=small[:, m_], in0=small[:, m_], scalar1=1.0 / 128, scalar2=None, op0=ALU.mult),
                         reads=[sk], writes=[sk])
                    S.op("dve", lambda e, m_=m_, x_=x_: e.tensor_tensor(out=small[:, x_], in0=small[:, m_], in1=small[:, m_], op=ALU.mult),
                         reads=[sk], writes=[sk])
                    S.op("dve", lambda e, v_=v_, x_=x_: e.scalar_tensor_tensor(out=small[:, v_], in0=small[:, v_], scalar=1.0 / 128,
                                                                            in1=small[:, x_], op0=ALU.mult, op1=ALU.subtract),
                         reads=[sk], writes=[sk])
                    S.op("dve", lambda e, v_=v_: e.tensor_scalar(out=small[:, v_], in0=small[:, v_], scalar1=EPS, scalar2=None, op0=ALU.add),
                         reads=[sk], writes=[sk])
                    S.op("pool", lambda e, v_=v_: e.tensor_tensor(out=small[:, v_], in0=small[:, v_], in1=mhalf[:, 0:8], op=ALU.pow),
                         reads=[sk, "mhalf"], writes=[sk])

                def c_epi_c(tb):
                    rsb, rkeys, rv = c_st[tb]["rsb"], c_st[tb]["rkeys"], c_st[tb]["rv"]
                    b0 = 16 if tb % 2 == 0 else 64
                    sk = "sC%d" % (tb % 2)
                    m_, v_, x_ = slice(b0, b0 + 8), slice(b0 + 8, b0 + 16), slice(b0 + 16, b0 + 24)
                    S.op("dve", lambda e, m_=m_, v_=v_, x_=x_: e.scalar_tensor_tensor(out=small[:, x_], in0=small[:, m_], scalar=-1.0,
                                                                                   in1=small[:, v_], op0=ALU.mult, op1=ALU.mult),
                         reads=[sk], writes=[sk])
                    hi = nxt("hs", 2)
                    rg = hs[hi]
                    rgk = "hs%d" % hi
                    rn = junk[:].rearrange("p (h e) -> p h e", h=NH)
                    for h in range(NH):
                        if h < 4:
                            S.op("act", lambda e, h=h, rv=rv, rn=rn, b0=b0: e.activation(
                                out=rn[:, h, :], in_=rv[:, h, :], func=AF.Identity,
                                scale=small[:, b0 + 8 + h:b0 + 9 + h], bias=small[:, b0 + 16 + h:b0 + 17 + h]),
                                reads=rkeys + [sk, "junk"], writes=["junk"])
                        else:
                            S.op("dve", lambda e, h=h, rv=rv, rn=rn, b0=b0: e.tensor_scalar(
                                out=rn[:, h, :], in0=rv[:, h, :], scalar1=small[:, b0 + h:b0 + h + 1],
                                scalar2=small[:, b0 + 8 + h:b0 + 9 + h], op0=ALU.subtract, op1=ALU.mult),
                                reads=rkeys + [sk], writes=["junkB"])
                    S.op("dve", lambda e, rg=rg, tb=tb: e.tensor_tensor(out=rg[:], in0=junk[:], in1=sg[:, tb, :], op=ALU.mult),
                         reads=["junk", "junkB", "B%d" % (tb * 2), "B%d" % (tb * 2 + 1)], writes=[rgk])
                    c_st[tb].update(rg=rg, rgk=rgk)

                def c_tr(tb):
                    rg, rgk, tsl = c_st[tb]["rg"], c_st[tb]["rgk"], c_st[tb]["tsl"]
                    pi = nxt("pa", 4)
                    ptv = pa[pi][:].bitcast(BF16).rearrange("p (k j) -> p k j", k=8)

                    def tr8b(e, rg=rg, ptv=ptv):
                        for kc in range(8):
                            ins = e.transpose(out=ptv[:, kc, :], in_=rg[:, kc * 128:(kc + 1) * 128], identity=ident[:])
                        return ins
                    S.op("pe", tr8b, reads=[rgk, "ident"], writes=["pa%d" % pi])
                    S.op("act", lambda e, ptv=ptv, tsl=tsl: e.copy(out=rgT[:, :, tsl], in_=ptv),
                         reads=["pa%d" % pi], writes=CK)

                mg = {}

                def gate_fill(c, col0):
                    hf, j = c // 4, c % 4
                    if j == 0:
                        mg["w"] = next_w(("in", col0 + hf * 512))
                    wm, wmk = mg["w"]
                    pi = proj_feat(wm, wmk, j, hT, HT)
                    S.op("act", lambda e, pi=pi, c=c: e.copy(out=smg[:, c, :], in_=pa[pi][:]), reads=["pa%d" % pi], writes=["G%d" % c])
                GK = ["G%d" % c for c in range(8)]

                c_scores(0); c_out(0); c_state(0)
                wg0, wg0k = next_w(("in", C_GR))
                for tb in range(4):
                    b_vg(1, 0, wg0, wg0k, tb)
                c_scores(1); c_epi_a(0); c_out(1); c_state(1); c_epi_b(0)
                wg1, wg1k = next_w(("in", C_GR + 512))
                for tb in range(4):
                    b_vg(1, 1, wg1, wg1k, tb)
                c_scores(2); c_epi_a(1); c_out(2); c_state(2); c_epi_c(0); c_epi_b(1)
                for c in range(4):
                    gate_fill(c, C_MR)
                c_scores(3); c_epi_a(2); c_out(3); c_state(3); c_epi_c(1); c_epi_b(2)
                for c in range(4, 8):
                    gate_fill(c, C_MR)
                gbw = {}

                def gb_fill(c):
                    hf, j = c // 4, c % 4
                    if j == 0:
                        gbw["w"] = next_w(("in", C_GB + hf * 512))
                    wt, wk = gbw["w"]
                    pi = proj_feat(wt, wk, j, hT, HT)
                    S.op("act", lambda e, pi=pi, c=c: e.activation(out=sgb[:, c, :], in_=pa[pi][:], func=AF.Sigmoid),
                         reads=["pa%d" % pi], writes=["A%d" % c])
                c_tr(0)
                gb_fill(0); gb_fill(1)
                c_epi_a(3); c_epi_c(2); c_epi_b(3)
                gb_fill(2); gb_fill(3)
                c_tr(1); c_epi_c(3)
                gb_fill(4); gb_fill(5); gb_fill(6); gb_fill(7)
                c_tr(2); c_tr(3)
                S.op("act", lambda e: e.activation(out=smg[:].rearrange("p c n -> p (c n)"), in_=smg[:].rearrange("p c n -> p (c n)"), func=AF.Sigmoid),
                     reads=GK, writes=GK)

                for hf in range(2):
                    wr, wrk = next_w(("ro", hf * 512))
                    for j in range(4):
                        c = hf * 4 + j
                        pi2 = proj_feat(wr, wrk, j, rgT, CK)
                        S.op("dve", lambda e, pi2=pi2, c=c: e.tensor_tensor(out=t1[:, c, :], in0=pa[pi2][:], in1=smg[:, c, :], op=ALU.mult),
                             reads=["pa%d" % pi2, "G%d" % c], writes=["t1_%d" % c])

                for hf in range(2):
                    wt, wk = next_w(("in", C_GA + hf * 512))
                    for j in range(4):
                        c = hf * 4 + j
                        pi = proj_feat(wt, wk, j, hT, HT)
                        S.op("dve", lambda e, pi=pi, c=c: e.tensor_tensor(out=u_ext[:, c, HALO:HALO + TT], in0=pa[pi][:], in1=sgb[:, c, :], op=ALU.mult),
                             reads=["pa%d" % pi, "A%d" % c], writes=["D%d" % c])
                        S.op("pool", lambda e, c=c, l=l: e.tensor_copy(out=u_ext[:, c, 0:HALO], in_=halo[:, l, c, :]),
                             reads=["halo%d" % l], writes=["D%d" % c])
                DKS = ["D%d" % c for c in range(8)]
                S.op("pool", lambda e, l=l: e.tensor_copy(out=halo[:, l, :, :], in_=u_ext[:, :, TT:TT + HALO]),
                     reads=DKS, writes=["halo%d" % l])
                e_st = {}

                def e_diag(c):
                    di = nxt("diag", 2)
                    S.op("dve", lambda e, di=di, c=c, l=l: e.tensor_tensor(
                        out=diag[di][:], in0=ident[:].unsqueeze(1).broadcast_to([128, KCONV, 128]),
                        in1=cw[:, l, c, :].unsqueeze(2).broadcast_to([128, KCONV, 128]), op=ALU.mult),
                        reads=["ident", "cw"], writes=["diag%d" % di])
                    e_st[c] = dict(di=di)
                e_diag(0)
                e_diag(1)
                for hf in range(2):
                    wt, wk = next_w(("in", C_GC + hf * 512))
                    for j in range(4):
                        c = hf * 4 + j
                        pi = proj_feat(wt, wk, j, hT, HT)
                        S.op("act", lambda e, pi=pi, c=c: e.activation(out=sgc[:, c, :], in_=pa[pi][:], func=AF.Silu),
                             reads=["pa%d" % pi], writes=["B%d" % c])
                bi_ln = nxt("pb", 2)

                def e_conv(c):
                    di = e_st[c]["di"]
                    pi = nxt("pa", 4)

                    def convmm(e, pi=pi, di=di, c=c):
                        for j in range(KCONV):
                            ins = e.matmul(pa[pi][:], lhsT=diag[di][:, j, :], rhs=u_ext[:, c, j:j + TT], start=(j == 0), stop=(j == KCONV - 1))
                        return ins
                    S.op("pe", convmm, reads=["diag%d" % di, "D%d" % c], writes=["pa%d" % pi])
                    S.op("act", lambda e, pi=pi, c=c, l=l: e.activation(out=cf[:, c, :], in_=pa[pi][:], func=AF.Identity,
                                                                     bias=vec[:, 1, l, c:c + 1], scale=1.0),
                         reads=["pa%d" % pi, "vec"], writes=["cf%d" % c])
                    ci = nxt("cb", 2)
                    S.op("act", lambda e, ci=ci, c=c: e.copy(out=cb[ci][:, 0, :], in_=cf[:, c, :]), reads=["cf%d" % c], writes=["cb%d_0" % ci])
                    S.op("act", lambda e, ci=ci, c=c: e.activation(out=cb[ci][:, 1, :], in_=cf[:, c, :], func=AF.Square),
                         reads=["cf%d" % c], writes=["cb%d_1" % ci])
                    e_st[c]["ci"] = ci

                def e_stat(c):
                    ci = e_st[c]["ci"]

                    def statmm(e, ci=ci, c=c, bi=bi_ln):
                        e.matmul(pb[bi][:, 0:512], lhsT=ones[:], rhs=cb[ci][:, 0, :], start=(c == 0), stop=(c == 7))
                        return e.matmul(pb[bi][:, 512:1024], lhsT=ones[:], rhs=cb[ci][:, 1, :], start=(c == 0), stop=(c == 7))
                    S.op("pe", statmm, reads=["ones", "cb%d_0" % ci, "cb%d_1" % ci], writes=["pb%d" % bi_ln])
                for c in range(8):
                    e_conv(c)
                    if c + 2 < 8:
                        e_diag(c + 2)
                    if c >= 1:
                        e_stat(c - 1)
                e_stat(7)
                bi = bi_ln
                S.op("dve", lambda e, bi=bi: e.tensor_scalar(out=mean_t[:], in0=pb[bi][:, 0:512], scalar1=1.0 / D, scalar2=None, op0=ALU.mult),
                     reads=["pb%d" % bi], writes=["mean_t"])
                S.op("dve", lambda e: e.tensor_tensor(out=rstd_t[:], in0=mean_t[:], in1=mean_t[:], op=ALU.mult),
                     reads=["mean_t"], writes=["rstd_t"])
                S.op("dve", lambda e, bi=bi: e.scalar_tensor_tensor(out=rstd_t[:], in0=pb[bi][:, 512:1024], scalar=1.0 / D, in1=rstd_t[:],
                                                                    op0=ALU.mult, op1=ALU.subtract),
                     reads=["pb%d" % bi, "rstd_t"], writes=["rstd_t"])
                S.op("act", lambda e: e.activation(out=rstd_t[:], in_=rstd_t[:], func=AF.Ln, bias=epsb[:, 0:1], scale=1.0),
                     reads=["rstd_t", "epsb"], writes=["rstd_t"])
                S.op("act", lambda e: e.activation(out=rstd_t[:], in_=rstd_t[:], func=AF.Exp, scale=-0.5),
                     reads=["rstd_t"], writes=["rstd_t"])
                ln_st = {}

                lnb = [(tmpn[0], "tmpn0"), (tmpn[1], "tmpn1"), (sm[0], "sm0"), (sm[1], "sm1")]

                def ln_a(c):
                    tb_, tk_ = lnb[c % 4]
                    S.op("pool", lambda e, c=c, tb_=tb_: e.tensor_tensor(out=tb_[:], in0=cf[:, c, :], in1=mean_t[:], op=ALU.subtract),
                         reads=["cf%d" % c, "mean_t"], writes=[tk_])
                    S.op("dve", lambda e, tb_=tb_: e.tensor_tensor(out=tb_[:], in0=tb_[:], in1=rstd_t[:], op=ALU.mult),
                         reads=[tk_, "rstd_t"], writes=[tk_])
                    S.op("act", lambda e, c=c, tb_=tb_, l=l: e.activation(out=tb_[:], in_=tb_[:], func=AF.Silu,
                                                                       scale=vec[:, 2, l, c:c + 1], bias=vec[:, 3, l, c:c + 1]),
                         reads=[tk_, "vec"], writes=[tk_])
                    ln_st[c] = (tb_, tk_)

                def ln_b(c):
                    tb_, tk_ = ln_st[c]
                    S.op("dve", lambda e, c=c, tb_=tb_: e.tensor_tensor(out=zT[:, c, :], in0=tb_[:], in1=sgc[:, c, :], op=ALU.mult),
                         reads=[tk_, "B%d" % c], writes=["C%d" % c])
                for c in range(8):
                    gate_fill(c, C_MC)
                    ln_a(c)
                    if c >= 2:
                        ln_b(c - 2)
                ln_b(6)
                ln_b(7)
                S.op("act", lambda e: e.activation(out=smg[:].rearrange("p c n -> p (c n)"), in_=smg[:].rearrange("p c n -> p (c n)"), func=AF.Sigmoid),
                     reads=GK, writes=GK)
                for hf in range(2):
                    wc_, wck = next_w(("co", hf * 512))
                    for j in range(4):
                        c = hf * 4 + j
                        pi2 = proj_feat(wc_, wck, j, zT, CK)
                        smi = nxt("sm", 2)
                        S.op("dve", lambda e, pi2=pi2, smi=smi, c=c: e.tensor_tensor(out=sm[smi][:], in0=pa[pi2][:], in1=smg[:, c, :], op=ALU.mult),
                             reads=["pa%d" % pi2, "G%d" % c], writes=["sm%d" % smi])
                        S.op("pool", lambda e, smi=smi, c=c: e.tensor_tensor(out=yT[:, c, :], in0=sm[smi][:], in1=t1[:, c, :], op=ALU.add),
                             reads=["sm%d" % smi, "t1_%d" % c], writes=["E%d" % c])

                wo0, wo0k = next_w(("o", 0))
                wo1, wo1k = next_w(("o", 512))
                f_st = {}

                def f_mm(tb):
                    bi = nxt("pb", 2)

                    def womm(e, bi=bi, tb=tb, wo0=wo0, wo1=wo1):
                        for hf, wt in ((0, wo0), (1, wo1)):
                            for kc in range(8):
                                ins = e.matmul(pb[bi][:, hf * 512:(hf + 1) * 512], lhsT=yT[:, kc, tb * 128:(tb + 1) * 128],
                                               rhs=wt[:, kc, :], start=(kc == 0), stop=(kc == 7))
                        return ins
                    S.op("pe", womm, reads=EK + [wo0k, wo1k], writes=["pb%d" % bi])
                    f_st[tb] = bi

                def f_post(tb):
                    bi = f_st[tb]
                    c0 = 40 + tb * 4
                    tq = tb % 2
                    tbuf = cf[:, 2 * tq:2 * tq + 2, :].rearrange("p a n -> p (a n)")
                    tkeys = ["cf%d" % (2 * tq), "cf%d" % (2 * tq + 1)]
                    S.op("act", lambda e, bi=bi, c0=c0: e.activation(out=junk[:], in_=pb[bi][:], func=AF.Square, accum_out=small[:, c0:c0 + 1]),
                         reads=["pb%d" % bi, "junk"], writes=["junk", "junkB", "sF%d" % tb])
                    S.op("dve", lambda e, c0=c0: e.tensor_scalar(out=small[:, c0 + 2:c0 + 3], in0=small[:, c0:c0 + 1], scalar1=1.0 / D, scalar2=EPS,
                                                                 op0=ALU.mult, op1=ALU.add), reads=["sF%d" % tb], writes=["sG%d" % tb])
                    S.op("pool", lambda e, c0=c0: e.tensor_tensor(out=small[:, c0 + 2:c0 + 3], in0=small[:, c0 + 2:c0 + 3], in1=mhalf[:, 0:1], op=ALU.pow),
                         reads=["sG%d" % tb, "mhalf"], writes=["sG%d" % tb])
                    S.op("dve", lambda e, bi=bi, c0=c0, ps_=ps_, tbuf=tbuf: e.scalar_tensor_tensor(
                        out=tbuf, in0=pb[bi][:], scalar=small[:, c0 + 2:c0 + 3], in1=pg[:, ps_, :], op0=ALU.mult, op1=ALU.mult),
                        reads=["pb%d" % bi, "sG%d" % tb, "pg%d" % ps_], writes=tkeys)
                    S.op("dve", lambda e, tb=tb, tbuf=tbuf: e.tensor_tensor(out=x_sb[:, tb, :], in0=x_sb[:, tb, :], in1=tbuf, op=ALU.add),
                         reads=tkeys + [XK[tb]], writes=[XK[tb]])
                nxt_stats = a_stats if l + 1 < L else (lambda tb: None)

                def early_hs(tb):
                    if l + 1 < L:
                        hi = nxt("hs", 2)
                        S.op("dve", lambda e, tb=tb, hi=hi: e.tensor_scalar(out=hs[hi][:], in0=x_sb[:, tb, :],
                                                                            scalar1=small[:, 8 + tb:9 + tb], scalar2=None, op0=ALU.mult),
                             reads=[XK[tb], "sB%d" % tb], writes=["hs%d" % hi])
                        carry[tb] = hi
                f_mm(0); f_mm(1); f_post(0); f_mm(2); f_post(1); nxt_stats(0); f_mm(3); f_post(2); nxt_stats(1); early_hs(0)
                f_post(3); nxt_stats(2); early_hs(1); nxt_stats(3)
            S.dma("sp", lambda e, t=t: e.dma_start(out=xv_out[:, t * 4:(t + 1) * 4, :], in_=x_sb[:]), reads=XK, key="xs")
        info = S.emit()
    return nc, info


_CACHE = {}


def _layout_inputs(L_sel, NT, x, pre_norm_g, w_in, w_ret_out, conv_w, conv_b, conv_ln_g, conv_ln_b, w_conv_out, w_o, post_norm_g):
    ls = list(L_sel)
    fm = lambda v: np.ascontiguousarray(v[ls].reshape(len(ls), 8, 128).transpose(2, 0, 1))
    vecs = np.ascontiguousarray(np.stack([fm(pre_norm_g), fm(conv_b), fm(conv_ln_g), fm(conv_ln_b)], axis=1))
    cwt = np.ascontiguousarray(conv_w[ls].transpose(2, 0, 1).reshape(8, 128, len(ls), KCONV).transpose(1, 2, 0, 3))
    cs, dqk, dm, _ = make_consts(NT * TT)
    shared = dict(w_in=np.ascontiguousarray(w_in[ls]), w_ro=np.ascontiguousarray(w_ret_out[ls]),
                  w_co=np.ascontiguousarray(w_conv_out[ls]), w_o=np.ascontiguousarray(w_o[ls]),
                  vecs=vecs, cw=cwt, post_g=np.ascontiguousarray(post_norm_g[ls]), cs=cs, dqk=dqk, dmask=dm)
    return shared


def run_layers(L_sel, xs, params, NT=SEQ // TT):
    key = (len(L_sel), NT)
    if key not in _CACHE:
        _CACHE[key] = build_nc(len(L_sel), NT)[0]
    nc = _CACHE[key]
    shared = _layout_inputs(L_sel, NT, None, **params)
    in_maps = [dict(shared, x=np.ascontiguousarray(xc)) for xc in xs]
    res = run_bass_kernel_spmd(nc, in_maps, core_ids=list(range(len(xs))))
    return [r["out"] for r in res.results]


FUSED = True


def kernel(x, pre_norm_g, w_in, w_ret_out, conv_w, conv_b, conv_ln_g, conv_ln_b, w_conv_out, w_o, post_norm_g):
    x = np.asarray(x, dtype=np.float32)
    params = dict(pre_norm_g=np.asarray(pre_norm_g, np.float32), w_in=np.asarray(w_in, np.float32),
                  w_ret_out=np.asarray(w_ret_out, np.float32), conv_w=np.asarray(conv_w, np.float32),
                  conv_b=np.asarray(conv_b, np.float32), conv_ln_g=np.asarray(conv_ln_g, np.float32),
                  conv_ln_b=np.asarray(conv_ln_b, np.float32), w_conv_out=np.asarray(w_conv_out, np.float32),
                  w_o=np.asarray(w_o, np.float32), post_norm_g=np.asarray(post_norm_g, np.float32))
    xs = [x[b] for b in range(x.shape[0])]
    if FUSED:
        outs = run_layers(list(range(DEPTH)), xs, params)
    else:
        for l in range(DEPTH):
            xs = run_layers([l], xs, params)
        outs = xs
    return np.stack(outs, axis=0).astype(np.float32)
```
